# Optimizing a Trainium2 kernel written in Bass

```python
import math
import jax, jax.numpy as jnp
from jax import lax
import numpy as np

D_MODEL = 1024
BATCH = 2
SEQ = 8192
DEPTH = 2

CONV_WIDTH = D_MODEL
CONV_KERNEL = 31
N_HEADS = 16
HEAD_DIM = 64
ATTN_WIDTH = N_HEADS * HEAD_DIM
MOBA_BLOCK = 256
MOBA_TOP_K = 3
Q_CHUNK = 32
ROPE_THETA = 10000.0
LN_EPS = 1e-5
NEG_INF = -1e30
DEEPNORM_ALPHA = (2 * DEPTH) ** 0.25
DEEPNORM_BETA = (8 * DEPTH) ** -0.25
SPLIT_SIZES = (CONV_WIDTH, CONV_WIDTH, CONV_WIDTH,
               ATTN_WIDTH, ATTN_WIDTH, ATTN_WIDTH, ATTN_WIDTH,
               D_MODEL, D_MODEL)
N_IN = sum(SPLIT_SIZES)
SPLIT_POINTS = tuple(int(v) for v in np.cumsum(SPLIT_SIZES)[:-1])

kernel_name = "hybrid_conformer_conv_moba_gated_deepnorm"


def layer_norm(x, g, b):
    xf = x.astype(jnp.float32)
    mu = jnp.mean(xf, axis=-1, keepdims=True)
    var = jnp.mean(jnp.square(xf - mu), axis=-1, keepdims=True)
    y = (xf - mu) * lax.rsqrt(var + LN_EPS)
    return (y * g.astype(jnp.float32) + b.astype(jnp.float32)).astype(x.dtype)


def rope(x, positions):
    half = HEAD_DIM // 2
    inv_freq = ROPE_THETA ** (-jnp.arange(half, dtype=jnp.float32) / half)
    ang = positions.astype(jnp.float32)[:, None] * inv_freq[None, :]
    cos, sin = jnp.cos(ang), jnp.sin(ang)
    xf = x.astype(jnp.float32)
    x1, x2 = xf[..., :half], xf[..., half:]
    out = jnp.concatenate([x1 * cos - x2 * sin, x2 * cos + x1 * sin], axis=-1)
    return out.astype(x.dtype)


def conformer_conv(a_val, a_glu, conv_w, conv_b, cln_g, cln_b, w_pw2):
    h = a_val * jax.nn.sigmoid(a_glu)
    h = lax.conv_general_dilated(
        h, conv_w[:, None, :].astype(h.dtype), window_strides=(1,),
        padding=[(CONV_KERNEL - 1, 0)],
        dimension_numbers=('NWC', 'WIO', 'NWC'),
        feature_group_count=CONV_WIDTH) + conv_b
    h = jax.nn.silu(layer_norm(h, cln_g, cln_b))
    return h @ w_pw2


def moba_attention(q, k, v):
    B, H, S, Dh = q.shape
    nb = -(-S // MOBA_BLOCK)
    pad = nb * MOBA_BLOCK - S
    k_blk = jnp.pad(k, ((0, 0), (0, 0), (0, pad), (0, 0))).reshape(B, H, nb, MOBA_BLOCK, Dh)
    v_blk = jnp.pad(v, ((0, 0), (0, 0), (0, pad), (0, 0))).reshape(B, H, nb, MOBA_BLOCK, Dh)
    k_mean = jnp.mean(k_blk.astype(jnp.float32), axis=3)
    n_sel = min(MOBA_TOP_K, nb)
    scale = 1.0 / math.sqrt(Dh)
    nc = S // Q_CHUNK
    q_chunks = q.reshape(B, H, nc, Q_CHUNK, Dh).transpose(2, 0, 1, 3, 4)
    b_idx = jnp.arange(B)[:, None, None, None]
    h_idx = jnp.arange(H)[None, :, None, None]
    blk_ids = jnp.arange(nb)

    def chunk_fn(args):
        q_c, c = args
        q_pos = c * Q_CHUNK + jnp.arange(Q_CHUNK)
        own = (c * Q_CHUNK) // MOBA_BLOCK
        gate = jnp.einsum('bhqd,bhnd->bhqn', q_c.astype(jnp.float32), k_mean)
        past = blk_ids < own
        gate = jnp.where(past[None, None, None, :], gate, NEG_INF)
        _, top_i = lax.top_k(gate, n_sel)
        valid = past[top_i]
        k_sel = k_blk[b_idx, h_idx, top_i]
        v_sel = v_blk[b_idx, h_idx, top_i]
        qs = q_c * scale
        s_past = jnp.einsum('bhqd,bhqnkd->bhqnk', qs, k_sel).astype(jnp.float32)
        s_past = jnp.where(valid[..., None], s_past, NEG_INF)
        s_past = s_past.reshape(B, H, Q_CHUNK, n_sel * MOBA_BLOCK)
        k_own = lax.dynamic_index_in_dim(k_blk, own, axis=2, keepdims=False)
        v_own = lax.dynamic_index_in_dim(v_blk, own, axis=2, keepdims=False)
        key_pos = own * MOBA_BLOCK + jnp.arange(MOBA_BLOCK)
        causal = key_pos[None, :] <= q_pos[:, None]
        s_own = jnp.einsum('bhqd,bhkd->bhqk', qs, k_own).astype(jnp.float32)
        s_own = jnp.where(causal[None, None], s_own, NEG_INF)
        p = jax.nn.softmax(jnp.concatenate([s_past, s_own], axis=-1), axis=-1)
        p_past = p[..., :n_sel * MOBA_BLOCK].reshape(B, H, Q_CHUNK, n_sel, MOBA_BLOCK).astype(v.dtype)
        p_own = p[..., n_sel * MOBA_BLOCK:].astype(v.dtype)
        return (jnp.einsum('bhqnk,bhqnkd->bhqd', p_past, v_sel)
                + jnp.einsum('bhqk,bhkd->bhqd', p_own, v_own))

    out = lax.map(chunk_fn, (q_chunks, jnp.arange(nc)))
    return out.transpose(1, 2, 0, 3, 4).reshape(B, H, S, Dh)


def hybrid_layer(x, w_in, b_in, conv_w, conv_b, cln_g, cln_b, w_pw2,
                 w_proj_a, w_proj_b, w_out, ln_g, ln_b, positions):
    B, S, _ = x.shape
    u = x @ w_in + b_in
    a_val, a_glu, a_z, q, k, v, b_z, g_a, g_b = jnp.split(u, SPLIT_POINTS, axis=-1)
    h_a = conformer_conv(a_val, a_glu, conv_w, conv_b, cln_g, cln_b, w_pw2) * jax.nn.silu(a_z)
    y_a = h_a @ w_proj_a
    to_heads = lambda t: t.reshape(B, S, N_HEADS, HEAD_DIM).transpose(0, 2, 1, 3)
    qh = rope(to_heads(q), positions)
    kh = rope(to_heads(k), positions)
    o = moba_attention(qh, kh, to_heads(v))
    o = o.transpose(0, 2, 1, 3).reshape(B, S, ATTN_WIDTH)
    y_b = (o * jax.nn.silu(b_z)) @ w_proj_b
    merged = jax.nn.sigmoid(g_a) * y_a + jax.nn.sigmoid(g_b) * y_b
    out = merged @ w_out
    return layer_norm(DEEPNORM_ALPHA * x + out, ln_g, ln_b)


def setup_inputs(seed: int = 0) -> dict:
    key = jax.random.key(seed)
    ks = jax.random.split(key, 14)
    f32 = jnp.float32
    nrm = lambda k, shape: jax.random.normal(k, shape, dtype=f32)
    return {
        "x": nrm(ks[0], (BATCH, SEQ, D_MODEL)),
        "w_in": nrm(ks[1], (DEPTH, D_MODEL, N_IN)) * D_MODEL ** -0.5,
        "b_in": 0.01 * nrm(ks[2], (DEPTH, N_IN)),
        "conv_w": nrm(ks[3], (DEPTH, CONV_KERNEL, CONV_WIDTH)) * CONV_KERNEL ** -0.5,
        "conv_b": 0.01 * nrm(ks[4], (DEPTH, CONV_WIDTH)),
        "conv_ln_g": 1.0 + 0.01 * nrm(ks[5], (DEPTH, CONV_WIDTH)),
        "conv_ln_b": 0.01 * nrm(ks[6], (DEPTH, CONV_WIDTH)),
        "w_pw2": nrm(ks[7], (DEPTH, CONV_WIDTH, CONV_WIDTH)) * CONV_WIDTH ** -0.5,
        "w_proj_a": nrm(ks[8], (DEPTH, CONV_WIDTH, D_MODEL)) * CONV_WIDTH ** -0.5 * DEEPNORM_BETA,
        "w_proj_b": nrm(ks[9], (DEPTH, ATTN_WIDTH, D_MODEL)) * ATTN_WIDTH ** -0.5 * DEEPNORM_BETA,
        "w_out": nrm(ks[10], (DEPTH, D_MODEL, D_MODEL)) * D_MODEL ** -0.5 * DEEPNORM_BETA,
        "ln_g": 1.0 + 0.01 * nrm(ks[11], (DEPTH, D_MODEL)),
        "ln_b": 0.01 * nrm(ks[12], (DEPTH, D_MODEL)),
    }


def reference(x, w_in, b_in, conv_w, conv_b, conv_ln_g, conv_ln_b, w_pw2,
              w_proj_a, w_proj_b, w_out, ln_g, ln_b):
    positions = jnp.arange(x.shape[1], dtype=jnp.int32)
    for l in range(DEPTH):
        x = hybrid_layer(x, w_in[l], b_in[l], conv_w[l], conv_b[l], conv_ln_g[l], conv_ln_b[l],
                         w_pw2[l], w_proj_a[l], w_proj_b[l], w_out[l], ln_g[l], ln_b[l], positions)
    return x
```

```python
import math
from contextlib import ExitStack

import numpy as np
import ml_dtypes

import concourse.bass as bass
import concourse.mybir as mybir
from concourse.bass_utils import run_bass_kernel_spmd

F32 = mybir.dt.float32
BF16 = mybir.dt.bfloat16
AF = mybir.ActivationFunctionType
ALU = mybir.AluOpType
AX = mybir.AxisListType

D = 1024
S = 8192
B = 2
DEPTH = 2
NH = 16
HD = 64
BLK = 256
NBLK = S // BLK
NLB = 8
TL = NLB * BLK
CK = 31
HALO = 32
TOPK = 3
LN_EPS = 1e-5
ALPHA = (2 * DEPTH) ** 0.25
NEGBIG = -32768.0
SAME_SYNC = True

OFF = dict(val=0, glu=1024, az=2048, q=3072, k=4096, v=5120, bz=6144, ga=7168, gb=8192)


class Buf:
    __slots__ = ("name", "w", "r")

    def __init__(self, name=""):
        self.name = name
        self.w = None
        self.r = {}


class K:
    def __init__(self, nc, st):
        self.nc = nc
        self.st = st
        self.E = {"pe": nc.tensor, "act": nc.scalar, "dve": nc.vector,
                  "pool": nc.gpsimd, "sp": nc.sync}
        self.sem = {n: st.enter_context(nc.semaphore("prog_" + n)) for n in self.E}
        self.cnt = {n: 0 for n in self.E}
        self.seen = {n: {} for n in self.E}
        self.dring = {}
        self.dpos = {}
        for q, n in (("sp", 32), ("pool", 16), ("act", 24)):
            self.dring[q] = [[st.enter_context(nc.semaphore("d_%s%d" % (q, i))), 0, None]
                             for i in range(n)]
            self.dpos[q] = 0
        self.n_inst = 0

    def sb(self, name, shape, dt):
        return self.st.enter_context(self.nc.sbuf_tensor(name, list(shape), dt))

    def ps(self, name, shape, dt=F32):
        return self.st.enter_context(self.nc.psum_tensor(name, list(shape), dt))

    def _wait(self, en, toks):
        need = {}
        for tok, kind in toks:
            if tok is None:
                continue
            sem, key, val, src = tok
            if src == en and not (kind == "raw" and SAME_SYNC):
                continue
            if self.seen[en].get(key, 0) >= val:
                continue
            if key not in need or need[key][1] < val:
                need[key] = (sem, val)
        for key, (sem, val) in need.items():
            self.E[en].wait_ge(sem, val)
            self.seen[en][key] = val

    @staticmethod
    def _deps(reads, writes):
        toks = []
        for b in reads:
            if b.w is not None:
                toks.append((b.w, "raw"))
        for b in writes:
            if b.w is not None:
                toks.append((b.w, "waw"))
            for t in b.r.values():
                toks.append((t, "war"))
        return toks

    @staticmethod
    def _reg(tok, reads, writes):
        key = tok[1]
        for b in reads:
            b.r[key] = tok
        for b in writes:
            b.w = tok
            b.r = {}

    def op(self, en, emit, reads=(), writes=()):
        self._wait(en, self._deps(reads, writes))
        inst = emit(self.E[en])
        self.cnt[en] += 1
        inst.then_inc(self.sem[en], 1)
        tok = (self.sem[en], en, self.cnt[en], en)
        self._reg(tok, reads, writes)
        self.n_inst += 1
        return tok

    def dma(self, q, out, in_, reads=(), writes=()):
        ring = self.dring[q]
        slot = ring[self.dpos[q] % len(ring)]
        self.dpos[q] += 1
        toks = self._deps(reads, writes)
        if slot[2] is not None:
            toks.append((slot[2], "raw"))
        self._wait(q, toks)
        inst = self.E[q].dma_start(out=out, in_=in_)
        slot[1] += 16
        inst.then_inc(slot[0], 16)
        tok = (slot[0], ("d", q, id(slot)), slot[1], None)
        slot[2] = tok
        self._reg(tok, reads, writes)
        self.n_inst += 1
        return tok

    def wait_all(self, en, bufs):
        toks = []
        for b in bufs:
            if b.w is not None:
                toks.append((b.w, "raw"))
            for t in b.r.values():
                toks.append((t, "raw"))
        self._wait(en, toks)

    def barrier(self, bufs_all=None):
        toks = []
        for n in ("pe", "act", "dve", "pool"):
            if self.cnt[n] > 0:
                toks.append(((self.sem[n], n, self.cnt[n], None), "raw"))
        for q in self.dring:
            for slot in self.dring[q]:
                if slot[2] is not None:
                    toks.append((slot[2], "raw"))
        for en in self.E:
            self._wait(en, toks)


def v3(ap, a):
    return ap.rearrange("p (a b) -> p a b", a=a)


A_ORDER = [("v", _c) for _c in range(8)]
for _c in range(8):
    A_ORDER += [("kp", _c), ("k", _c)]
KV_DONE_AT = len(A_ORDER)
for _c in range(8):
    A_ORDER += [("glu", _c), ("val", _c)]
HOOK_AT = len(A_ORDER)
for _c in range(8):
    A_ORDER += [("qp", _c), ("q", _c)]
for _f in ("az", "bz", "ga", "gb"):
    for _c in range(8):
        A_ORDER.append((_f, _c))
N_WCH = len(A_ORDER)
N_ACH = N_WCH


def phase_A(k, io, pfx):
    nc = k.nc
    x, w_in_r, bias_col, b_v = io["x"], io["w_in_r"], io["bias_col"], io["b_v"]
    cos_d, sins_d, ident_d = io["cos"], io["sins"], io["ident_f"]
    with ExitStack() as st:
        sb = lambda n, s, d: st.enter_context(nc.sbuf_tensor(pfx + n, list(s), d))
        xT = sb("xT", [128, 8, TL], BF16)
        xTb = [Buf("xT%d" % t) for t in range(16)]
        xin = [sb("xin%d" % i, [128, D], F32) for i in range(4)]
        xinb = [Buf() for _ in range(4)]
        wf = [sb("wf%d" % i, [128, D], F32) for i in range(4)]
        wfb = [Buf() for _ in range(4)]
        wb = [sb("wb%d" % i, [128, 8, 128], BF16) for i in range(2)]
        wbb = [Buf() for _ in range(2)]
        T1 = [sb("T1_%d" % i, [128, TL], F32) for i in range(2)]
        T1b = [Buf() for _ in range(2)]
        T2 = [sb("T2_%d" % i, [128, TL], F32) for i in range(2)]
        T2b = [Buf() for _ in range(2)]
        Kf = sb("Kf", [128, TL], F32)
        Kfb = Buf()
        SG = sb("SG", [128, TL], F32)
        SGb = Buf()
        OB = [sb("OB%d" % i, [128, TL], BF16) for i in range(3)]
        OBb = [Buf() for _ in range(3)]
        cos_t = sb("cos", [128, TL], F32)
        sin_t = sb("sin", [128, TL], F32)
        tabb = Buf()
        wv = sb("wv", [128, 8, D], BF16)
        wvb = Buf()
        Vst = sb("Vst", [128, NH, 16, 65], BF16)
        Vstb = Buf()
        bias_t = sb("bias", [128, N_ACH], F32)
        biasb = Buf()
        ident = sb("ident", [128, 128], F32)
        identb = Buf()
        bvf = sb("bvf", [1, D], F32)
        bvb16 = sb("bvb16", [1, D], BF16)
        bvb = Buf()
        ones1 = sb("ones1", [1, 128], BF16)
        ones1b = Buf()
        km = [sb("km%d" % i, [128, 8], F32) for i in range(2)]
        km2 = [sb("kmm%d" % i, [128, 8], F32) for i in range(2)]
        kmb = [Buf() for _ in range(2)]
        km2b = [Buf() for _ in range(2)]
        ps = [st.enter_context(nc.psum_tensor(pfx + "ps%d" % i, [128, 512], F32)) for i in range(8)]
        psb = [Buf("ps%d" % i) for i in range(8)]

        k.dma("sp", ident[:, :], ident_d[:, :], writes=[identb])
        k.dma("sp", bias_t[:, :], bias_col[:, :], writes=[biasb])
        k.dma("sp", cos_t[:, :], cos_d[:, :], writes=[tabb])
        k.dma("sp", sin_t[:, :], sins_d[:, :], writes=[tabb])
        k.dma("sp", bvf[:, :], b_v[:, :], writes=[bvb])
        k.op("pool", lambda e: e.tensor_copy(out=bvb16[:, :], in_=bvf[:, :]), reads=[bvb], writes=[bvb])
        k.op("pool", lambda e: e.memset(ones1[:, :], 1.0), writes=[ones1b])
        k.op("pool", lambda e: e.memset(Vst[:, :, :, :].rearrange("p a b c -> p (a b c)"), 1.0), writes=[Vstb])

        def load_chunk(ci, i):
            s = i % 4
            k.dma("sp", wf[s][:, :], w_in_r[ci], writes=[wfb[s]])
        load_chunk(0, 0)
        load_chunk(1, 1)

        ev = 0
        for tt in range(16):
            s = tt % 4
            k.dma("sp", xin[s][:, :], x[tt * 128:(tt + 1) * 128, :], writes=[xinb[s]])
            for half in range(2):
                bk = (tt * 2 + half) % 8

                def emit(e, s=s, half=half, bk=bk):
                    last = None
                    for j in range(4):
                        kk = half * 4 + j
                        last = e.transpose(out=ps[bk][:, j * 128:(j + 1) * 128],
                                           in_=xin[s][:, kk * 128:(kk + 1) * 128],
                                           identity=ident[:, :])
                    return last
                k.op("pe", emit, reads=[xinb[s], identb], writes=[psb[bk]])
                dst = xT[:, half * 4:(half + 1) * 4, tt * 128:(tt + 1) * 128]
                src = v3(ps[bk][:, :], 4)
                if ev % 2 == 0:
                    k.op("act", lambda e, dst=dst, src=src: e.activation(out=dst, in_=src, func=AF.Copy),
                         reads=[psb[bk]], writes=[xTb[tt]])
                else:
                    k.op("dve", lambda e, dst=dst, src=src: e.tensor_copy(out=dst, in_=src),
                         reads=[psb[bk]], writes=[xTb[tt]])
                ev += 1


        ob_i = [0]

        def next_ob():
            i = ob_i[0] % 3
            ob_i[0] += 1
            return i

        toks = []
        toks_h = []
        def do_V():
            for tt in range(16):
                for half in range(2):
                    bk = (tt * 2 + half) % 8

                    def emit(e, tt=tt, half=half, bk=bk):
                        for kk in range(8):
                            e.matmul(out=ps[bk][:, :], lhsT=xT[:, kk, tt * 128:(tt + 1) * 128],
                                     rhs=wv[:, kk, half * 512:(half + 1) * 512], start=(kk == 0), stop=False)
                        return e.matmul(out=ps[bk][:, :], lhsT=ones1[0:1, :], rhs=bvb16[0:1, half * 512:(half + 1) * 512],
                                        start=False, stop=True)
                    k.op("pe", emit, reads=[xTb[tt], wvb, ones1b, bvb], writes=[psb[bk]])
                    dst = Vst[:, half * 8:(half + 1) * 8, tt, 0:HD]
                    src = v3(ps[bk][:, :], 8)
                    if (tt + half) % 2 == 0:
                        k.op("act", lambda e, dst=dst, src=src: e.activation(out=dst, in_=src, func=AF.Copy),
                             reads=[psb[bk]], writes=[Vstb])
                    else:
                        k.op("dve", lambda e, dst=dst, src=src: e.tensor_copy(out=dst, in_=src),
                             reads=[psb[bk]], writes=[Vstb])
            for pc in range(8):
                toks.append(k.dma("pool", io["Vp_piece"](pc), Vst[:, 2 * pc:2 * pc + 2, :, :].rearrange("p a b c -> p (a b c)"), reads=[Vstb]))

        def cast_chunk(i):
            kind, c = A_ORDER[i]
            s = i % 2
            f = i % 4
            if kind == "v":
                k.op("act", lambda e: e.activation(out=wv[:, :, c * 128:(c + 1) * 128], in_=v3(wf[f][:, :], 8), func=AF.Copy),
                     reads=[wfb[f]], writes=[wvb])
            else:
                k.op("act", lambda e: e.activation(out=wb[s][:, :, :].rearrange("p a b -> p (a b)"), in_=wf[f][:, :], func=AF.Copy),
                     reads=[wfb[f]], writes=[wbb[s]])

        load_chunk(2, 2)
        load_chunk(3, 3)
        cast_chunk(0)
        for i, (kind, c) in enumerate(A_ORDER):
            s = i % 2
            if i == KV_DONE_AT + 4 and io.get("hook_kv") is not None:
                io["hook_kv"](toks)
            if i == HOOK_AT + 4 and io.get("hook") is not None:
                io["hook"](toks_h)
            if i + 1 < N_WCH:
                cast_chunk(i + 1)
            if i + 4 < N_WCH:
                load_chunk(i + 4, i + 4)
            if kind == "v":
                if c == 7:
                    do_V()
                continue
            banks = [(i % 2) * 4 + g for g in range(4)]
            for g in range(4):
                bk = banks[g]

                def emit(e, s=s, g=g, bk=bk):
                    last = None
                    for kk in range(8):
                        last = e.matmul(out=ps[bk][:, :], lhsT=wb[s][:, kk, :],
                                        rhs=xT[:, kk, g * 512:(g + 1) * 512],
                                        start=(kk == 0), stop=(kk == 7))
                    return last
                k.op("pe", emit, reads=[wbb[s]] + xTb[4 * g:4 * g + 4], writes=[psb[bk]])
            bias = bias_t[:, i:i + 1]
            t = (i // 2) % 2
            if kind in ("qp", "kp"):
                for g in range(4):
                    bk = banks[g]
                    k.op("dve", lambda e, bk=bk, g=g, t=t, bias=bias: e.scalar_tensor_tensor(
                        out=T2[t][:, g * 512:(g + 1) * 512], in0=ps[bk][:, :], scalar=bias,
                        in1=sin_t[:, g * 512:(g + 1) * 512], op0=ALU.add, op1=ALU.mult),
                        reads=[psb[bk], biasb, tabb], writes=[T2b[t]])
            elif kind in ("q", "k"):
                for g in range(4):
                    bk = banks[g]
                    k.op("dve", lambda e, bk=bk, g=g, t=t, bias=bias: e.scalar_tensor_tensor(
                        out=T1[t][:, g * 512:(g + 1) * 512], in0=ps[bk][:, :], scalar=bias,
                        in1=cos_t[:, g * 512:(g + 1) * 512], op0=ALU.add, op1=ALU.mult),
                        reads=[psb[bk], biasb, tabb], writes=[T1b[t]])
                o = next_ob()
                if kind == "q":
                    k.op("pool", lambda e, t=t: e.tensor_tensor(out=T1[t][:, :], in0=T1[t][:, :], in1=T2[t][:, :], op=ALU.add),
                         reads=[T1b[t], T2b[t]], writes=[T1b[t]])
                    k.op("act", lambda e, o=o, t=t: e.activation(out=OB[o][:, :], in_=T1[t][:, :], func=AF.Copy),
                         reads=[T1b[t]], writes=[OBb[o]])
                    k.dma("act", io["qT"][c * 128:(c + 1) * 128, :], OB[o][:, :], reads=[OBb[o]])
                    o2 = next_ob()
                    k.op("dve", lambda e, t=t, o=o, o2=o2: e.tensor_tensor(out=OB[o2][:, :], in0=T1[t][:, :], in1=OB[o][:, :], op=ALU.subtract),
                         reads=[T1b[t], OBb[o]], writes=[OBb[o2]])
                    k.dma("sp", io["qlo"][c * 128:(c + 1) * 128, :], OB[o2][:, :], reads=[OBb[o2]])
                else:
                    k.op("pool", lambda e, t=t: e.tensor_tensor(out=Kf[:, :], in0=T1[t][:, :], in1=T2[t][:, :], op=ALU.add),
                         reads=[T1b[t], T2b[t]], writes=[Kfb])
                    k.op("act", lambda e, o=o: e.activation(out=OB[o][:, :], in_=Kf[:, :], func=AF.Copy),
                         reads=[Kfb], writes=[OBb[o]])
                    toks.append(k.dma("act", io["KT_rows"](c), OB[o][:, :], reads=[OBb[o]]))
                    m_ = c % 2
                    k.op("dve", lambda e, m_=m_: e.tensor_reduce(out=km[m_][:, :], in_=v3(Kf[:, :], 8), axis=AX.X, op=ALU.add),
                         reads=[Kfb], writes=[kmb[m_]])
                    k.op("dve", lambda e, m_=m_: e.tensor_scalar(out=km2[m_][:, :], in0=km[m_][:, :], scalar1=1.0 / BLK, scalar2=None, op0=ALU.mult),
                         reads=[kmb[m_]], writes=[km2b[m_]])
                    toks.append(k.dma("act", io["kmean"][c * 128:(c + 1) * 128, :], km2[m_][:, :], reads=[km2b[m_]]))
            elif kind == "glu":
                for g in range(4):
                    bk = banks[g]
                    k.op("act", lambda e, bk=bk, g=g, bias=bias: e.activation(
                        out=SG[:, g * 512:(g + 1) * 512], in_=ps[bk][:, :], func=AF.Sigmoid, bias=bias, scale=1.0),
                        reads=[psb[bk], biasb], writes=[SGb])
            elif kind == "val":
                o = next_ob()
                for g in range(4):
                    bk = banks[g]
                    k.op("dve", lambda e, bk=bk, g=g, o=o, bias=bias: e.scalar_tensor_tensor(
                        out=OB[o][:, g * 512:(g + 1) * 512], in0=ps[bk][:, :], scalar=bias,
                        in1=SG[:, g * 512:(g + 1) * 512], op0=ALU.add, op1=ALU.mult),
                        reads=[psb[bk], biasb, SGb], writes=[OBb[o]])
                k.dma("act", io["hT"][c * 128:(c + 1) * 128, :], OB[o][:, :], reads=[OBb[o]])
                toks_h.append(k.dma("act", v3(io["htail"][c * 128:(c + 1) * 128, :], 8),
                                  v3(OB[o][:, :], 8)[:, :, BLK - HALO:BLK], reads=[OBb[o]]))
            else:
                fn = AF.Silu if kind in ("az", "bz") else AF.Sigmoid
                dst = {"az": "saz", "bz": "sbz", "ga": "sga", "gb": "sgb"}[kind]
                o = next_ob()
                for g in range(4):
                    bk = banks[g]
                    k.op("act", lambda e, bk=bk, g=g, o=o, bias=bias, fn=fn: e.activation(
                        out=OB[o][:, g * 512:(g + 1) * 512], in_=ps[bk][:, :], func=fn, bias=bias, scale=1.0),
                        reads=[psb[bk], biasb], writes=[OBb[o]])
                k.dma("act", io[dst][c * 128:(c + 1) * 128, :], OB[o][:, :], reads=[OBb[o]])

        k.barrier()


def _bf(a):
    return np.ascontiguousarray(a).astype(ml_dtypes.bfloat16)


def core_tokens(j):
    lt = np.arange(TL)
    return (4 * (lt // BLK) + j) * BLK + (lt % BLK)


def prep_A_weights(w_in_l, b_in_l):
    cols = []
    for kind, c in A_ORDER:
        p = np.arange(128)
        if kind in ("qp", "kp"):
            base = OFF["q"] if kind == "qp" else OFF["k"]
            head = 2 * c + p // 64
            d = p % 64
            cols.append(base + head * 64 + (d + 32) % 64)
        else:
            cols.append(OFF[kind] + c * 128 + p)
    cols = np.stack(cols)
    w = w_in_l[:, cols]
    w = w.reshape(8, 128, N_WCH, 128).transpose(2, 1, 0, 3)
    w_in_r = np.ascontiguousarray(w.reshape(N_WCH, 128, 1024), dtype=np.float32)
    bias_col = np.ascontiguousarray(b_in_l[cols].T, dtype=np.float32)
    b_v = np.ascontiguousarray(b_in_l[OFF["v"]:OFF["v"] + D][None, :], dtype=np.float32)
    return w_in_r, bias_col, b_v


def rope_tables(j):
    pos = core_tokens(j).astype(np.float64)
    p = np.arange(128)
    d = p % 64
    f = d % 32
    inv = 10000.0 ** (-f.astype(np.float64) / 32.0)
    ang = inv[:, None] * pos[None, :]
    ang32 = (pos.astype(np.float32)[None, :] * (10000.0 ** (-(f.astype(np.float32)) / 32.0)).astype(np.float32)[:, None]).astype(np.float32)
    cos = np.cos(ang32.astype(np.float64)).astype(np.float32)
    sin = np.sin(ang32.astype(np.float64)).astype(np.float32)
    sign = np.where(d < 32, -1.0, 1.0).astype(np.float32)[:, None]
    return cos, sin * sign


def phase_B(k, io, pfx):
    nc = k.nc
    with ExitStack() as st0:
        sb0 = lambda n, s, d: st0.enter_context(nc.sbuf_tensor(pfx + n, list(s), d))
        ps = [st0.enter_context(nc.psum_tensor(pfx + "ps%d" % i, [128, 512], F32)) for i in range(8)]
        psb = [Buf("ps%d" % i) for i in range(8)]
        Ma = sb0("Ma", [128, 8, TL], BF16)
        Mab = [Buf("Ma%d" % g) for g in range(4)]
        identb_t = sb0("identb", [128, 128], BF16)
        negI = sb0("negI", [128, 128], BF16)
        onesbb = sb0("onesbb", [128, 128], BF16)
        onesf = sb0("onesf", [128, 64], F32)
        cst = Buf("consts")
        wst = [sb0("wst%d" % i, [128, D], F32) for i in range(2)]
        wstb = [Buf() for _ in range(2)]
        wctr = [0]
        k.dma("sp", identb_t[:, :], io["ident_b"][:, :], writes=[cst])
        k.dma("sp", negI[:, :], io["negI"][:, :], writes=[cst])
        k.op("pool", lambda e: e.memset(onesbb[:, :], 1.0), writes=[cst])
        k.op("pool", lambda e: e.memset(onesf[:, :], 1.0), writes=[cst])

        def load_weight(wi, dst, dstb):
            for kk in range(8):
                s = wctr[0] % 2
                wctr[0] += 1
                k.dma("sp", wst[s][:, :], io["w4"][wi, :, kk, :], writes=[wstb[s]])
                k.op("act", lambda e, s=s, kk=kk: e.activation(out=dst[:, kk, :], in_=wst[s][:, :], func=AF.Copy),
                     reads=[wstb[s]], writes=[dstb])

        Kaug0 = sb0("Kaug0", [128, 4, TL], BF16)
        Qaug0 = sb0("Qaug0", [128, TL], BF16)
        Kb0, Qb0, Qab0 = Buf(), Buf(), Buf()

        def early_head0():
            if io.get("pre_attn") is not None:
                io["pre_attn"]()
            k.dma("sp", Kaug0[64:96, :, :].rearrange("p a b -> p (a b)"), io["ind"][:, :], writes=[Kb0])
            k.dma("sp", Qaug0[0:64, :], io["qT"][0:64, :], writes=[Qb0])
            for r in range(4):
                k.dma("sp", Kaug0[0:64, r, :], io["KT_all_head"](r, 0), writes=[Kb0])

        with ExitStack() as st1:
            sb1 = lambda n, s, d: st1.enter_context(nc.sbuf_tensor(pfx + n, list(s), d))
            Cbuf = sb1("Cbuf", [128, 8, TL], BF16)
            Cb = [[Buf() for _ in range(4)] for _ in range(8)]
            cw_t = sb1("cw", [128, 8, CK], F32)
            cvec = sb1("cvec", [128, 3, 8], F32)
            cvb = Buf()
            k.dma("sp", cw_t[:, :, :], io["cw"][:, :, :], writes=[cvb])
            k.dma("sp", cvec[:, :, :], io["cvec"][:, :, :], writes=[cvb])
            wpw2 = sb1("wpw2", [128, 8, D], BF16)
            wpa = sb1("wpa", [128, 8, D], BF16)
            wpw2b, wpab = Buf(), Buf()
            with ExitStack() as st2:
                sb2 = lambda n, s, d: st2.enter_context(nc.sbuf_tensor(pfx + n, list(s), d))
                hbuf = sb2("hbuf", [128, 8, NLB, HALO + BLK], BF16)
                hb = [Buf() for _ in range(8)]
                halob = Buf()
                cand = sb2("cand", [128, 4, 8, NLB, HALO], BF16)
                candb = Buf()
                acc = sb2("acc", [128, 8 * NLB * HALO], F32)
                accb = Buf()
                sel4 = sb2("sel4", [128, 4], F32)
                diag = [sb2("diag%d" % i, [128, CK, 128], BF16) for i in range(2)]
                diagb = [Buf() for _ in range(2)]
                diagb2 = [Buf() for _ in range(2)]
                k.dma("sp", sel4[:, :], io["sel4"][:, :], writes=[candb])
                k.op("pool", lambda e: e.memset(cand[:, 3, :, :, :].rearrange("p a b c -> p (a b c)"), 0.0), writes=[candb])
                for r in range(3):
                    k.dma("sp", cand[:, r, :, :, :].rearrange("p a b c -> p a (b c)"),
                          io["htail_all"][r * D:(r + 1) * D, :].rearrange("(c p) f -> p c f", p=128), writes=[candb])
                k.dma("sp", cand[:, 3, :, 1:NLB, :],
                      io["htail_all"][3 * D:4 * D, :].rearrange("(c p) (m f) -> p c m f", p=128, f=HALO)[:, :, 0:NLB - 1, :],
                      writes=[candb])
                for c in range(8):
                    k.dma("sp", hbuf[:, c, :, HALO:HALO + BLK],
                          io["hT"][c * 128:(c + 1) * 128, :].rearrange("p (m t) -> p m t", m=NLB), writes=[hb[c]])
                if io.get("kv_gather") is not None:
                    io["kv_gather"]()
                cflat = lambda r: cand[:, r, :, :, :].rearrange("p a b c -> p (a b c)")
                k.op("dve", lambda e: e.tensor_scalar(out=acc[:, :], in0=cflat(0), scalar1=sel4[:, 0:1], scalar2=None, op0=ALU.mult),
                     reads=[candb], writes=[accb])
                for r in range(1, 4):
                    k.op("dve", lambda e, r=r: e.scalar_tensor_tensor(out=acc[:, :], in0=cflat(r), scalar=sel4[:, r:r + 1],
                                                                     in1=acc[:, :], op0=ALU.mult, op1=ALU.add),
                         reads=[candb, accb], writes=[accb])
                k.op("dve", lambda e: e.tensor_copy(
                    out=hbuf[:, :, :, 0:HALO].rearrange("p a b c -> p (a b) c"),
                    in_=acc[:, :].rearrange("p (a c) -> p a c", c=HALO)),
                    reads=[accb], writes=[halob])
                for c in range(8):
                    s = c % 2
                    if c == 2:
                        load_weight(0, wpw2, wpw2b)
                    if c == 4:
                        load_weight(1, wpa, wpab)
                    for j in range(CK):
                        if j % 2 == 0:
                            k.op("act", lambda e, s=s, c=c, j=j: e.activation(
                                out=diag[s][:, j, :], in_=identb_t[:, :], func=AF.Copy, scale=cw_t[:, c, j:j + 1]),
                                reads=[cst, cvb], writes=[diagb[s]])
                        else:
                            k.op("dve", lambda e, s=s, c=c, j=j: e.tensor_scalar(
                                out=diag[s][:, j, :], in0=identb_t[:, :], scalar1=cw_t[:, c, j:j + 1], scalar2=None, op0=ALU.mult),
                                reads=[cst, cvb], writes=[diagb2[s]])
                    for pr in range(4):
                        bk = (c * 4 + pr) % 8

                        def emit(e, s=s, c=c, pr=pr, bk=bk):
                            last = None
                            for j in range(CK):
                                last = e.matmul(out=ps[bk][:, :], lhsT=diag[s][:, j, :],
                                                rhs=hbuf[:, c, 2 * pr:2 * pr + 2, j + 2:j + 2 + BLK],
                                                start=(j == 0), stop=(j == CK - 1))
                            return last
                        k.op("pe", emit, reads=[diagb[s], diagb2[s], hb[c], halob], writes=[psb[bk]])
                        k.op("dve", lambda e, c=c, pr=pr, bk=bk: e.tensor_scalar(
                            out=Cbuf[:, c, pr * 512:(pr + 1) * 512], in0=ps[bk][:, :], scalar1=cvec[:, 0, c:c + 1],
                            scalar2=None, op0=ALU.add), reads=[psb[bk], cvb], writes=[Cb[c][pr]])
                k.barrier()
            with ExitStack() as st2:
                sb2 = lambda n, s, d: st2.enter_context(nc.sbuf_tensor(pfx + n, list(s), d))
                sq = [sb2("sq%d" % i, [128, 512], BF16) for i in range(2)]
                sqb = [Buf() for _ in range(2)]
                mu = [sb2("mu%d" % i, [128, 512], F32) for i in range(2)]
                msq = sb2("msq", [128, 512], F32)
                var = sb2("var", [128, 512], F32)
                rstd = [sb2("rstd%d" % i, [128, 512], F32) for i in range(2)]
                mub = [Buf() for _ in range(2)]
                rstdb = [Buf() for _ in range(2)]
                msqb, varb = Buf(), Buf()
                tmp = [sb2("tmp%d" % i, [128, 512], F32) for i in range(2)]
                tmpb = [Buf() for _ in range(2)]
                tmq = [sb2("tmq%d" % i, [128, 512], F32) for i in range(2)]
                tmqb = [Buf() for _ in range(2)]
                Nbuf = [sb2("Nbuf%d" % i, [128, 8, 512], BF16) for i in range(2)]
                Nb = [[Buf() for _ in range(8)] for _ in range(2)]
                Gbuf = sb2("Gbuf", [128, 8, 512], BF16)
                Gb = [Buf() for _ in range(8)]
                sazg = [sb2("sazg%d" % i, [128, 8, 512], BF16) for i in range(2)]
                sgag = [sb2("sgag%d" % i, [128, 8, 512], BF16) for i in range(2)]
                sazb = [Buf() for _ in range(2)]
                sgab = [Buf() for _ in range(2)]
                epsb_t = sb2("eps", [128, 1], F32)
                k.op("pool", lambda e: e.memset(epsb_t[:, :], LN_EPS), writes=[cst])
                GS = [slice(g * 512, (g + 1) * 512) for g in range(4)]
                early_head0()

                def stage_S(g):
                    gs = GS[g]
                    q2 = g % 2
                    k.dma("sp", sazg[q2][:, :, :], io["saz"].rearrange("(c p) t -> p c t", p=128)[:, :, gs], writes=[sazb[q2]])
                    k.dma("sp", sgag[q2][:, :, :], io["sga"].rearrange("(c p) t -> p c t", p=128)[:, :, gs], writes=[sgab[q2]])

                    def emit(e):
                        last = None
                        for c in range(8):
                            last = e.matmul(out=ps[0][:, :], lhsT=onesbb[:, :], rhs=Cbuf[:, c, gs], start=(c == 0), stop=(c == 7))
                        return last
                    k.op("pe", emit, reads=[cst] + [Cb[c][g] for c in range(8)], writes=[psb[0]])
                    for c in range(8):
                        s = c % 2
                        k.op("act", lambda e, s=s, c=c: e.activation(out=sq[s][:, :], in_=Cbuf[:, c, gs], func=AF.Square),
                             reads=[Cb[c][g]], writes=[sqb[s]])
                        k.op("pe", lambda e, s=s, c=c: e.matmul(out=ps[1][:, :], lhsT=onesbb[:, :], rhs=sq[s][:, :],
                                                              start=(c == 0), stop=(c == 7)),
                             reads=[cst, sqb[s]], writes=[psb[1]])
                    k.op("dve", lambda e: e.tensor_scalar(out=mu[q2][:, :], in0=ps[0][:, :], scalar1=1.0 / D, scalar2=None, op0=ALU.mult),
                         reads=[psb[0]], writes=[mub[q2]])
                    k.op("dve", lambda e: e.tensor_tensor(out=msq[:, :], in0=mu[q2][:, :], in1=mu[q2][:, :], op=ALU.mult),
                         reads=[mub[q2]], writes=[msqb])
                    k.op("dve", lambda e: e.scalar_tensor_tensor(out=var[:, :], in0=ps[1][:, :], scalar=1.0 / D, in1=msq[:, :],
                                                                 op0=ALU.mult, op1=ALU.subtract),
                         reads=[psb[1], msqb], writes=[varb])
                    k.op("act", lambda e: e.activation(out=var[:, :], in_=var[:, :], func=AF.Sqrt, bias=epsb_t[:, 0:1], scale=1.0),
                         reads=[varb, cst], writes=[varb])
                    k.op("dve", lambda e: e.reciprocal(out=rstd[q2][:, :], in_=var[:, :]), reads=[varb], writes=[rstdb[q2]])

                def stage_N(g, c):
                    gs = GS[g]
                    q2 = g % 2
                    s = c % 2
                    k.op("dve", lambda e: e.tensor_tensor(out=tmp[s][:, :], in0=Cbuf[:, c, gs], in1=mu[q2][:, :], op=ALU.subtract),
                         reads=[Cb[c][g], mub[q2]], writes=[tmpb[s]])
                    k.op("pool", lambda e: e.tensor_tensor(out=tmq[s][:, :], in0=tmp[s][:, :], in1=rstd[q2][:, :], op=ALU.mult),
                         reads=[tmpb[s], rstdb[q2]], writes=[tmqb[s]])
                    k.op("act", lambda e: e.activation(out=Nbuf[q2][:, c, :], in_=tmq[s][:, :], func=AF.Silu,
                                                       bias=cvec[:, 2, c:c + 1], scale=cvec[:, 1, c:c + 1]),
                         reads=[tmqb[s], cvb], writes=[Nb[q2][c]])

                def stage_P(g, c2):
                    q2 = g % 2
                    bk = 2 + (c2 % 3)

                    def emit(e):
                        last = None
                        for kk in range(8):
                            last = e.matmul(out=ps[bk][:, :], lhsT=wpw2[:, kk, c2 * 128:(c2 + 1) * 128], rhs=Nbuf[q2][:, kk, :],
                                            start=(kk == 0), stop=(kk == 7))
                        return last
                    k.op("pe", emit, reads=[wpw2b] + Nb[q2], writes=[psb[bk]])
                    k.op("dve", lambda e: e.tensor_tensor(out=Gbuf[:, c2, :], in0=ps[bk][:, :], in1=sazg[q2][:, c2, :], op=ALU.mult),
                         reads=[psb[bk], sazb[q2]], writes=[Gb[c2]])

                def stage_A(g, c2):
                    gs = GS[g]
                    q2 = g % 2
                    bk = 5 + (c2 % 3)

                    def emit(e):
                        last = None
                        for kk in range(8):
                            last = e.matmul(out=ps[bk][:, :], lhsT=wpa[:, kk, c2 * 128:(c2 + 1) * 128], rhs=Gbuf[:, kk, :],
                                            start=(kk == 0), stop=(kk == 7))
                        return last
                    k.op("pe", emit, reads=[wpab] + Gb, writes=[psb[bk]])
                    k.op("dve", lambda e: e.tensor_tensor(out=Ma[:, c2, gs], in0=ps[bk][:, :], in1=sgag[q2][:, c2, :], op=ALU.mult),
                         reads=[psb[bk], sgab[q2]], writes=[Mab[g]])

                stage_S(0)
                for c in range(8):
                    stage_N(0, c)
                for g in range(4):
                    if g + 1 < 4:
                        stage_S(g + 1)
                    for c2 in range(8):
                        stage_P(g, c2)
                        if g + 1 < 4:
                            stage_N(g + 1, c2)
                    for c2 in range(8):
                        stage_A(g, c2)
                k.barrier()

        oT = io["oT"]
        oTb = Buf("oT")
        stw = st0.enter_context(ExitStack())
        wpb = stw.enter_context(nc.sbuf_tensor(pfx + "wpb", [128, 8, D], BF16))
        wo = stw.enter_context(nc.sbuf_tensor(pfx + "wo", [128, 8, D], BF16))
        wpbb, wob = Buf(), Buf()
        with ExitStack() as st1:
            sb1 = lambda n, s, d: st1.enter_context(nc.sbuf_tensor(pfx + n, list(s), d))
            Kaug = [Kaug0, sb1("Kaug1", [128, 4, TL], BF16)]
            Vaug = [sb1("Vaug%d" % i, [128, 4, 16, 65], BF16) for i in range(2)]
            Qaug = [Qaug0, sb1("Qaug1", [128, TL], BF16)]
            Kb = [Kb0, Buf()]
            Vb = [Buf() for _ in range(2)]
            Qb = [Qb0, Buf()]
            Qab = [Qab0, Buf()]
            qlo = [sb1("qlo%d" % i, [64, TL], BF16) for i in range(2)]
            qlob = [Buf() for _ in range(2)]
            kml = [sb1("kml%d" % i, [64, 32], BF16) for i in range(2)]
            kmt = [sb1("kmt%d" % i, [64, 32], F32) for i in range(2)]
            kmlb = [Buf() for _ in range(2)]
            kmf = [sb1("kmf%d" % i, [64, 32], F32) for i in range(2)]
            kmbf = [sb1("kmbf%d" % i, [64, 32], BF16) for i in range(2)]
            kmfb = [Buf() for _ in range(2)]
            kmbb = [Buf() for _ in range(2)]
            pastb = sb1("pastb", [128, 512], F32)
            notown = sb1("notown", [128, 512], F32)
            masks = sb1("masks", [128, 4, 512], BF16)
            g0 = sb1("g0", [128, 512], F32)
            g1 = sb1("g1", [128, 512], F32)
            g2 = sb1("g2", [128, 512], F32)
            eq = sb1("eq", [128, 512], F32)
            mx = sb1("mx", [128, 16], F32)
            g0b, g1b, g2b, eqb, mxb = Buf(), Buf(), Buf(), Buf(), Buf()
            b01 = sb1("b01", [128, 16, 96], BF16)
            b01b = Buf()
            Pt = [sb1("Pt%d" % i, [128, 512], BF16) for i in range(16)]
            Ptb = [Buf() for _ in range(16)]
            rs = sb1("rs", [128, 512], F32)
            osb = [sb1("osb%d" % i, [64, 512], F32) for i in range(2)]
            rsb = Buf()
            osbb = [Buf() for _ in range(2)]
            ohead = [sb1("ohead%d" % i, [64, TL], BF16) for i in range(2)]
            oheadb = [Buf() for _ in range(2)]
            if io.get("pre_attn") is not None:
                io["pre_attn"]()
            k.dma("sp", pastb[:, :], io["pastb"][:, :], writes=[cst])
            k.dma("sp", notown[:, :], io["notown"][:, :], writes=[cst])
            k.dma("sp", masks[:, :, :], io["masks"][:, :, :], writes=[cst])
            k.op("pool", lambda e: e.memset(b01[:, :, :].rearrange("p a b -> p (a b)"), 0.0), writes=[b01b])
            k.dma("sp", Kaug[1][64:96, :, :].rearrange("p a b -> p (a b)"), io["ind"][:, :], writes=[Kb[1]])
            SB = [0, 1, 2, 3, 4]
            OBK = [5, 6]
            GBK, TBK, BCK = 7, 7, 7

            def prep1(h):
                s = h % 2
                k.dma("sp", qlo[s][:, :], io["qlo"][h * 64:(h + 1) * 64, :], writes=[qlob[s]])
                if h > 0:
                    k.dma("sp", Qaug[s][0:64, :], io["qT"][h * 64:(h + 1) * 64, :], writes=[Qb[s]])
                    for r in range(4):
                        k.dma("sp", Kaug[s][0:64, r, :], io["KT_all_head"](r, h), writes=[Kb[s]])
                k.dma("sp", Vaug[s][:, :, :, :].rearrange("p a b c -> p a (b c)"), io["Vp_all_head"](h), writes=[Vb[s]])
                k.dma("sp", kmf[s][:, :].rearrange("p (r m) -> p r m", r=4),
                      io["km_all"].rearrange("(r f) m -> f r m", r=4)[h * 64:(h + 1) * 64, :, :], writes=[kmfb[s]])
                k.op("dve", lambda e: e.tensor_copy(out=kmbf[s][:, :], in_=kmf[s][:, :]), reads=[kmfb[s]], writes=[kmbb[s]])
                k.op("dve", lambda e: e.tensor_tensor(out=kmt[s][:, :], in0=kmf[s][:, :], in1=kmbf[s][:, :], op=ALU.subtract),
                     reads=[kmfb[s], kmbb[s]], writes=[kmlb[s]])
                k.op("dve", lambda e: e.tensor_copy(out=kml[s][:, :], in_=kmt[s][:, :]), reads=[kmlb[s]], writes=[kmlb[s]])

            def prep1b(h):
                s = h % 2

                def emit(e):
                    last = None
                    for tt in range(16):
                        o_ = ps[GBK][:, tt * 32:(tt + 1) * 32]
                        ts_ = slice(tt * 128, (tt + 1) * 128)
                        e.matmul(out=o_, lhsT=Qaug[s][0:64, ts_], rhs=kmbf[s][:, :], start=True, stop=False)
                        e.matmul(out=o_, lhsT=qlo[s][0:64, ts_], rhs=kmbf[s][:, :], start=False, stop=False)
                        last = e.matmul(out=o_, lhsT=Qaug[s][0:64, ts_], rhs=kml[s][:, :], start=False, stop=True)
                    return last
                k.op("pe", emit, reads=[Qb[s], qlob[s], kmbb[s], kmlb[s]], writes=[psb[GBK]])
                V3 = lambda t: t[:, :].rearrange("p (a b) -> p a b", a=16)
                bc = lambda: mx[:, :].unsqueeze(2).broadcast_to([128, 16, 32])
                k.op("dve", lambda e: e.tensor_tensor(out=g0[:, :], in0=ps[GBK][:, :], in1=pastb[:, :], op=ALU.add),
                     reads=[psb[GBK], cst], writes=[g0b])
                src, srcb = g0, g0b
                for (dst, dstb) in ((g1, g1b), (g2, g2b)):
                    k.op("dve", lambda e, src=src: e.tensor_reduce(out=mx[:, :], in_=V3(src), axis=AX.X, op=ALU.max),
                         reads=[srcb], writes=[mxb])
                    k.op("dve", lambda e, src=src: e.tensor_tensor(out=V3(eq), in0=V3(src), in1=bc(), op=ALU.is_ge),
                         reads=[srcb, mxb], writes=[eqb])
                    k.op("dve", lambda e, src=src, dst=dst: e.scalar_tensor_tensor(out=dst[:, :], in0=eq[:, :], scalar=-1e30, in1=src[:, :],
                                                                                 op0=ALU.mult, op1=ALU.add),
                         reads=[eqb, srcb], writes=[dstb])
                    src, srcb = dst, dstb
                k.op("dve", lambda e: e.tensor_reduce(out=mx[:, :], in_=V3(g2), axis=AX.X, op=ALU.max), reads=[g2b], writes=[mxb])
                k.op("dve", lambda e: e.tensor_scalar(out=mx[:, :], in0=mx[:, :], scalar1=-1e29, scalar2=None, op0=ALU.max),
                     reads=[mxb], writes=[mxb])
                k.op("dve", lambda e: e.tensor_tensor(out=V3(eq), in0=V3(g0), in1=bc(), op=ALU.is_lt), reads=[g0b, mxb], writes=[eqb])
                k.op("dve", lambda e: e.tensor_tensor(out=b01[:, :, 64:96], in0=V3(eq), in1=V3(notown), op=ALU.mult),
                     reads=[eqb, cst], writes=[b01b])

            def prep2(h):
                s = h % 2
                for grp in range(4):
                    def emit(e, grp=grp):
                        last = None
                        for i in range(4):
                            tt = grp * 4 + i
                            last = e.matmul(out=ps[TBK][0:96, i * 128:(i + 1) * 128], lhsT=b01[:, tt, :], rhs=negI[:, :],
                                            start=True, stop=True)
                        return last
                    k.op("pe", emit, reads=[b01b, cst], writes=[psb[TBK]])
                    k.op("dve", lambda e, grp=grp: e.tensor_copy(out=Qaug[s][64:96, grp * 512:(grp + 1) * 512], in_=ps[TBK][64:96, :]),
                         reads=[psb[TBK]], writes=[Qab[s]])

            items = []
            for h in range(NH):
                for P in range(NLB // 2):
                    for mm in range(2 * P + 2):
                        for r in range(4):
                            for kt in range(2):
                                items.append((h, P, mm, r, kt))
            per_head = len(items) // NH
            SKEW = 3
            pend_pv = []
            deferred = []

            GRP = 2

            def do_score(i0):
                grp = [items[i0 + t] for t in range(GRP)]
                s = grp[0][0] % 2

                def emit(e):
                    last = None
                    for t, (h, P, mm, r, kt) in enumerate(grp):
                        bk = SB[(i0 + t) % 5]
                        q0 = P * 512 + (256 if mm == 2 * P + 1 else 0)
                        w = (P + 1) * 512 - q0
                        last = e.matmul(out=ps[bk][:, 0:w],
                                        lhsT=Kaug[s][0:96, r, mm * 256 + kt * 128:mm * 256 + (kt + 1) * 128],
                                        rhs=Qaug[s][0:96, q0:q0 + w], start=True, stop=True)
                    return last
                k.op("pe", emit, reads=[Kb[s], Qb[s], Qab[s]], writes=[psb[SB[(i0 + t) % 5]] for t in range(GRP)])
                for t, (h, P, mm, r, kt) in enumerate(grp):
                    bk = SB[(i0 + t) % 5]
                    p = (i0 + t) % 16
                    w = 256 if mm == 2 * P + 1 else 512
                    k.op("act", lambda e, bk=bk, p=p, w=w: e.activation(out=Pt[p][:, 0:w], in_=ps[bk][:, 0:w], func=AF.Exp, scale=1.0 / math.sqrt(HD)),
                         reads=[psb[bk]], writes=[Ptb[p]])
                    if mm >= 2 * P:
                        off = 0
                        k.op("dve", lambda e, p=p, off=off, r=r, kt=kt: e.tensor_tensor(
                            out=Pt[p][:, off:off + 256], in0=Pt[p][:, off:off + 256],
                            in1=masks[:, r, kt * 256:(kt + 1) * 256], op=ALU.mult),
                            reads=[Ptb[p], cst], writes=[Ptb[p]])

            def do_pv(i0):
                grp = [items[i0 + t] for t in range(GRP)]
                h, P = grp[0][0], grp[0][1]
                s = h % 2
                hp = h * (NLB // 2) + P
                ob = OBK[hp % 2]
                flags = []
                for (h_, P_, mm, r, kt) in grp:
                    assert (h_, P_) == (h, P)
                    flags.append(((mm == 0 and r == 0 and kt == 0), (mm == 2 * P + 1 and r == 3 and kt == 1)))

                def emit(e):
                    last = None
                    for t, (h_, P_, mm, r, kt) in enumerate(grp):
                        p = (i0 + t) % 16
                        o0 = 256 if mm == 2 * P + 1 else 0
                        last = e.matmul(out=ps[ob][0:65, o0:512], lhsT=Vaug[s][:, r, mm * 2 + kt, :], rhs=Pt[p][:, 0:512 - o0],
                                        start=flags[t][0], stop=flags[t][1])
                    return last
                k.op("pe", emit, reads=[Vb[s]] + [Ptb[(i0 + t) % 16] for t in range(GRP)], writes=[psb[ob]])
                if flags[-1][1]:
                    o2 = hp % 2
                    hs = h % 2
                    k.op("dve", lambda e: e.reciprocal(out=rs[64:65, :], in_=ps[ob][64:65, :]), reads=[psb[ob]], writes=[rsb])
                    k.op("dve", lambda e: e.tensor_copy(out=osb[o2][:, :], in_=ps[ob][0:64, :]),
                         reads=[psb[ob]], writes=[osbb[o2]])

                    def fin():
                        k.op("pe", lambda e: e.matmul(out=ps[BCK][0:64, :], lhsT=onesf[64:65, 0:64], rhs=rs[64:65, :],
                                                      start=True, stop=True),
                             reads=[rsb, cst], writes=[psb[BCK]])
                        k.op("dve", lambda e: e.tensor_tensor(out=ohead[hs][:, P * 512:(P + 1) * 512], in0=osb[o2][:, :],
                                                              in1=ps[BCK][0:64, :], op=ALU.mult),
                             reads=[osbb[o2], psb[BCK]], writes=[oheadb[hs]])
                        if P == NLB // 2 - 1:
                            k.dma("pool", oT[h * 64:(h + 1) * 64, :], ohead[hs][:, :], reads=[oheadb[hs]], writes=[oTb])
                    deferred.append([8, fin])

            def tick():
                for d in deferred:
                    d[0] -= 1
                while deferred and deferred[0][0] <= 0:
                    deferred.pop(0)[1]()

            prep1(0)
            prep1b(0)
            prep2(0)
            SKG = 4
            ngrp = len(items) // GRP
            for gi in range(ngrp):
                i = gi * GRP
                h = items[i][0]
                li = i - h * per_head
                if li == 24 and h == 1:
                    load_weight(2, wpb, wpbb)
                if li == 24 and h == 2:
                    load_weight(3, wo, wob)
                if li == 4 * SKG and h + 1 < NH:
                    prep1(h + 1)
                if li == 60 and h + 1 < NH:
                    prep1b(h + 1)
                if li == 112 and h + 1 < NH:
                    prep2(h + 1)
                do_score(i)
                if gi - SKG >= 0:
                    do_pv((gi - SKG) * GRP)
                tick()
            for gi in range(ngrp - SKG, ngrp):
                do_pv(gi * GRP)
                tick()
            while deferred:
                deferred.pop(0)[1]()
            k.barrier()

        with ExitStack() as st1:
            sb1 = lambda n, s, d: st1.enter_context(nc.sbuf_tensor(pfx + n, list(s), d))
            og = [sb1("og%d" % i, [128, 8, 512], BF16) for i in range(2)]
            sbzg = [sb1("sbzg%d" % i, [128, 8, 512], BF16) for i in range(2)]
            sgbg = [sb1("sgbg%d" % i, [128, 8, 512], BF16) for i in range(2)]
            OBg = [sb1("OBg%d" % i, [128, 8, 512], BF16) for i in range(2)]
            mg = [sb1("mg%d" % i, [128, 8, 512], BF16) for i in range(2)]
            ogb = [Buf() for _ in range(2)]
            sbzb = [Buf() for _ in range(2)]
            sgbb = [Buf() for _ in range(2)]
            OBgb = [Buf() for _ in range(2)]
            mgb = [[Buf() for _ in range(8)] for _ in range(2)]
            tmpf = [sb1("tmpf%d" % i, [128, 512], F32) for i in range(2)]
            tmpfb = [Buf() for _ in range(2)]
            xt = [sb1("xt%d" % i, [128, D], F32) for i in range(2)]
            xtb = [Buf() for _ in range(2)]
            Z = [sb1("Z%d" % i, [128, D], F32) for i in range(2)]
            Zb = [Buf() for _ in range(2)]
            Y = Z
            Yb = Zb
            lng = sb1("lng", [128, D], F32)
            lnb_t = sb1("lnb", [128, D], F32)
            stt = [sb1("stt%d" % i, [128, 2, 6], F32) for i in range(2)]
            mv = [sb1("mv%d" % i, [128, 2], F32) for i in range(2)]
            rsd = [sb1("rsd%d" % i, [128, 1], F32) for i in range(2)]
            sttb = [Buf() for _ in range(2)]
            mvb = [Buf() for _ in range(2)]
            rsdb = [Buf() for _ in range(2)]
            eps2 = sb1("eps2", [128, 1], F32)
            k.op("pool", lambda e: e.memset(eps2[:, :], LN_EPS), writes=[cst])
            k.dma("sp", lng[:, :], io["lng"][:, :], writes=[cst])
            k.dma("sp", lnb_t[:, :], io["lnb"][:, :], writes=[cst])
            y_out = io["y"]
            fl = lambda t: t[:, :, :].rearrange("p a b -> p (a b)")
            GS = [slice(g * 512, (g + 1) * 512) for g in range(4)]

            def stage_L(g):
                q2 = g % 2
                gs = GS[g]
                k.dma("sp", og[q2][:, :, :], oT.rearrange("(c p) t -> p c t", p=128)[:, :, gs], reads=[oTb], writes=[ogb[q2]])
                k.dma("sp", sbzg[q2][:, :, :], io["sbz"].rearrange("(c p) t -> p c t", p=128)[:, :, gs], writes=[sbzb[q2]])
                k.dma("sp", sgbg[q2][:, :, :], io["sgb"].rearrange("(c p) t -> p c t", p=128)[:, :, gs], writes=[sgbb[q2]])
                k.op("pool", lambda e: e.tensor_tensor(out=fl(OBg[q2]), in0=fl(og[q2]), in1=fl(sbzg[q2]), op=ALU.mult),
                     reads=[ogb[q2], sbzb[q2]], writes=[OBgb[q2]])

            def stage_P(g, c2):
                q2 = g % 2
                gs = GS[g]
                bk = c2 % 3
                s = c2 % 2

                def emit(e):
                    last = None
                    for kk in range(8):
                        last = e.matmul(out=ps[bk][:, :], lhsT=wpb[:, kk, c2 * 128:(c2 + 1) * 128], rhs=OBg[q2][:, kk, :],
                                        start=(kk == 0), stop=(kk == 7))
                    return last
                k.op("pe", emit, reads=[wpbb, OBgb[q2]], writes=[psb[bk]])
                k.op("dve", lambda e: e.tensor_tensor(out=tmpf[s][:, :], in0=ps[bk][:, :], in1=sgbg[q2][:, c2, :], op=ALU.mult),
                     reads=[psb[bk], sgbb[q2]], writes=[tmpfb[s]])
                k.op("pool", lambda e: e.tensor_tensor(out=mg[q2][:, c2, :], in0=tmpf[s][:, :], in1=Ma[:, c2, gs], op=ALU.add),
                     reads=[tmpfb[s], Mab[g]], writes=[mgb[q2][c2]])

            def stage_O(g, t4):
                q2 = g % 2
                tt = g * 4 + t4
                s = tt % 2
                k.dma("sp", xt[s][:, :], io["x"][tt * 128:(tt + 1) * 128, :], writes=[xtb[s]])
                for half in range(2):
                    bk = 3 + (tt * 2 + half) % 4
                    hs_ = slice(half * 512, (half + 1) * 512)

                    def emit(e, hs_=hs_, bk=bk):
                        last = None
                        for kk in range(8):
                            last = e.matmul(out=ps[bk][:, :], lhsT=mg[q2][:, kk, t4 * 128:(t4 + 1) * 128], rhs=wo[:, kk, hs_],
                                            start=(kk == 0), stop=(kk == 7))
                        return last
                    k.op("pe", emit, reads=[wob] + mgb[q2], writes=[psb[bk]])
                    k.op("dve", lambda e, hs_=hs_, bk=bk: e.scalar_tensor_tensor(
                        out=Z[s][:, hs_], in0=xt[s][:, hs_], scalar=float(ALPHA), in1=ps[bk][:, :], op0=ALU.mult, op1=ALU.add),
                        reads=[xtb[s], psb[bk]], writes=[Zb[s]])
                    k.op("dve", lambda e, hs_=hs_, half=half: e.bn_stats(out=stt[s][:, half, :], in_=Z[s][:, hs_]),
                         reads=[Zb[s]], writes=[sttb[s]])
                k.op("dve", lambda e: e.bn_aggr(out=mv[s][:, :], in_=stt[s][:, :, :].rearrange("p a b -> p (a b)")),
                     reads=[sttb[s]], writes=[mvb[s]])
                k.op("act", lambda e: e.activation(out=rsd[s][:, :], in_=mv[s][:, 1:2], func=AF.Sqrt, bias=eps2[:, 0:1], scale=1.0),
                     reads=[mvb[s], cst], writes=[rsdb[s]])
                k.op("dve", lambda e: e.reciprocal(out=rsd[s][:, :], in_=rsd[s][:, :]), reads=[rsdb[s]], writes=[rsdb[s]])
                k.op("dve", lambda e: e.tensor_scalar(out=Y[s][:, :], in0=Z[s][:, :], scalar1=mv[s][:, 0:1], scalar2=rsd[s][:, 0:1],
                                                      op0=ALU.subtract, op1=ALU.mult),
                     reads=[Zb[s], mvb[s], rsdb[s]], writes=[Yb[s]])
                k.op("dve", lambda e: e.tensor_tensor(out=Y[s][:, :], in0=Y[s][:, :], in1=lng[:, :], op=ALU.mult),
                     reads=[Yb[s], cst], writes=[Yb[s]])
                k.op("pool", lambda e: e.tensor_tensor(out=Y[s][:, :], in0=Y[s][:, :], in1=lnb_t[:, :], op=ALU.add),
                     reads=[Yb[s], cst], writes=[Yb[s]])
                k.dma("pool", y_out[tt * 128:(tt + 1) * 128, :], Y[s][:, :], reads=[Yb[s]])

            stage_L(0)
            for c2 in range(8):
                stage_P(0, c2)
            for g in range(4):
                if g + 1 < 4:
                    stage_L(g + 1)
                for t4 in range(4):
                    stage_O(g, t4)
                    if g + 1 < 4:
                        stage_P(g + 1, 2 * t4)
                        stage_P(g + 1, 2 * t4 + 1)
            k.barrier()

def core_consts(j):
    sel4 = np.zeros((128, 4), np.float32)
    sel4[:, (j - 1) % 4] = 1.0
    npr = np.arange(32)
    nglob = 4 * (npr % 8) + npr // 8
    pastb = np.zeros((16, 32), np.float32)
    notown = np.ones((16, 32), np.float32)
    for tt in range(16):
        own = 4 * (tt // 2) + j
        pastb[tt, nglob >= own] = -1e30
        notown[tt, nglob == own] = 0.0
    pastb = np.broadcast_to(pastb.reshape(1, 512), (128, 512)).copy()
    notown = np.broadcast_to(notown.reshape(1, 512), (128, 512)).copy()
    masks = np.zeros((128, 4, 2, 256), np.float32)
    p = np.arange(128)[:, None]
    q = np.arange(256)[None, :]
    for r in range(4):
        if r < j:
            masks[:, r] = 1.0
        elif r == j:
            for kt in range(2):
                masks[:, r, kt] = ((kt * 128 + p) <= q).astype(np.float32)
    return dict(sel4=sel4, pastb=pastb, notown=notown, masks=_bf(masks.reshape(128, 4, 512)))


def shared_consts():
    ind = np.zeros((32, 4, TL), np.float32)
    for r in range(4):
        for mm in range(8):
            ind[r * 8 + mm, r, mm * BLK:(mm + 1) * BLK] = 1.0
    return dict(ind=_bf(ind.reshape(32, 4 * TL)), ident_b=_bf(np.eye(128, dtype=np.float32)),
                negI=_bf(np.eye(128, dtype=np.float32) * NEGBIG))


def prep_B_weights(conv_w, conv_b, cln_g, cln_b, w_pw2, w_proj_a, w_proj_b, w_out, ln_g, ln_b):
    cw = np.ascontiguousarray(conv_w.T.reshape(8, 128, CK).transpose(1, 0, 2), dtype=np.float32)
    cvec = np.stack([v.reshape(8, 128).T for v in (conv_b, cln_g, cln_b)], axis=1)
    cvec = np.ascontiguousarray(cvec, dtype=np.float32)
    w4 = np.stack([w.reshape(8, 128, D).transpose(1, 0, 2) for w in (w_pw2, w_proj_a, w_proj_b, w_out)])
    w4 = np.ascontiguousarray(w4, dtype=np.float32)
    lng = np.ascontiguousarray(np.broadcast_to(ln_g[None, :], (128, D)), dtype=np.float32)
    lnb = np.ascontiguousarray(np.broadcast_to(ln_b[None, :], (128, D)), dtype=np.float32)
    return dict(cw=cw, cvec=cvec, w4=w4, lng=lng, lnb=lnb)


A_IN_L = [("w_in_r", [N_WCH, 128, 1024], F32), ("bias_col", [128, N_ACH], F32), ("b_v", [1, D], F32)]
B_IN_L = [("cw", [128, 8, CK], F32), ("cvec", [128, 3, 8], F32), ("w4", [4, 128, 8, D], F32),
          ("lng", [128, D], F32), ("lnb", [128, D], F32)]
SHARED_IN = [("x", [TL, D], F32), ("cos", [128, TL], F32), ("sins", [128, TL], F32), ("ident_f", [128, 128], F32),
             ("ident_b", [128, 128], BF16), ("negI", [128, 128], BF16), ("ind", [32, 4 * TL], BF16),
             ("sel4", [128, 4], F32), ("pastb", [128, 512], F32), ("notown", [128, 512], F32), ("masks", [128, 4, 512], BF16)]
GROUPS = [[0, 1, 2, 3], [4, 5, 6, 7]]


def build_fused():
    nc = bass.Bass("TRN2", target_bir_lowering=False)
    ext = {}
    for n, s, d in SHARED_IN:
        ext[n] = nc.dram_tensor(n, list(s), d, kind="ExternalInput").ap()
    for l in range(DEPTH):
        for n, s, d in A_IN_L + B_IN_L:
            ext["%s%d" % (n, l)] = nc.dram_tensor("%s%d" % (n, l), list(s), d, kind="ExternalInput").ap()
    y = nc.dram_tensor("y", [TL, D], F32, kind="ExternalOutput").ap()
    scr = lambda n, s, d: nc.dram_tensor(n, list(s), d).ap()
    with ExitStack() as st:
        k = K(nc, st)
        ccsem = st.enter_context(nc.semaphore("ccsem"))
        ccsem_h = st.enter_context(nc.semaphore("ccsem_h"))
        ccn = 0
        xcur = ext["x"]
        for l in range(DEPTH):
            sfx = "_%d" % l
            own = {n: scr(n + sfx, [D, TL], BF16) for n in ("qT", "hT", "saz", "sbz", "sga", "sgb")}
            own["qlo"] = scr("qlo" + sfx, [D, TL], BF16)
            own["kmean"] = scr("kmean" + sfx, [D, NLB], F32)
            own["htail"] = scr("htail" + sfx, [D, NLB * HALO], BF16)
            KTp = [scr("KT%d" % i + sfx, [256, TL], BF16) for i in range(4)]
            Vpp = [scr("Vp%d" % i + sfx, [128, 2 * 1040], BF16) for i in range(8)]
            ioA = dict(own)
            ioA["KT_rows"] = lambda c, KTp=KTp: KTp[c // 2][(c % 2) * 128:(c % 2 + 1) * 128, :]
            ioA["Vp_piece"] = lambda pc, Vpp=Vpp: Vpp[pc][:, :]
            ioA.update(x=xcur, cos=ext["cos"], sins=ext["sins"], ident_f=ext["ident_f"])
            for n, _, _ in A_IN_L:
                ioA[n] = ext["%s%d" % (n, l)]
            ioB = {}

            def gather(src, name, shp, dt, sfx, first=False):
                nonlocal ccn
                dst = scr(name + sfx, shp, dt)
                if first:
                    inst = nc.gpsimd.collective_compute("AllGather", ALU.bypass, replica_groups=GROUPS,
                                                        ins=[src.opt()], outs=[dst.opt()], dma_qos="P3")
                    inst.then_inc(ccsem_h)
                else:
                    inst = nc.gpsimd.collective_compute("AllGather", ALU.bypass, replica_groups=GROUPS,
                                                        ins=[src.opt()], outs=[dst.opt()], dma_qos="P3")
                    inst.then_inc(ccsem)
                    ccn += 1
                return dst

            def hook(toks, sfx=sfx, own=own, ioB=ioB):
                k._wait("pool", [(t, "raw") for t in toks])
                ioB["htail_all"] = gather(own["htail"], "htall", [4 * D, NLB * HALO], BF16, sfx, first=True)

            def kv_gather(toks, sfx=sfx, own=own, KTp=KTp, Vpp=Vpp, ioB=ioB):
                k._wait("pool", [(t, "raw") for t in toks])
                ioB["km_all"] = gather(own["kmean"], "kmall", [4 * D, NLB], F32, sfx)
                KTa = [gather(KTp[i], "KTall%d" % i, [4 * 256, TL], BF16, sfx) for i in range(4)]
                Vpa = [gather(Vpp[i], "Vpall%d" % i, [512, 2 * 1040], BF16, sfx) for i in range(8)]
                ioB["KT_all_head"] = lambda r, h, KTa=KTa: KTa[h // 4][r * 256 + (h % 4) * 64:r * 256 + (h % 4 + 1) * 64, :]
                ioB["Vp_all_head"] = lambda h, Vpa=Vpa: Vpa[h // 2].rearrange("(r p) f -> p r f", p=128)[:, :, (h % 2) * 1040:(h % 2 + 1) * 1040]
            ioA["hook_kv"] = kv_gather
            ioA["hook"] = hook
            phase_A(k, ioA, "a%d_" % l)
            for en in k.E:
                k.E[en].wait_ge(ccsem_h, l + 1)

            def pre_attn():
                for en in k.E:
                    k.E[en].wait_ge(ccsem, ccn)
            ioB["pre_attn"] = pre_attn
            for n in ("qT", "qlo", "hT", "saz", "sbz", "sga", "sgb"):
                ioB[n] = own[n]
            for n, _, _ in B_IN_L:
                ioB[n] = ext["%s%d" % (n, l)]
            for n in ("ident_b", "negI", "ind", "sel4", "pastb", "notown", "masks"):
                ioB[n] = ext[n]
            ioB["x"] = xcur
            ioB["oT"] = scr("oT" + sfx, [D, TL], BF16)
            if l == DEPTH - 1:
                ioB["y"] = y
            else:
                ioB["y"] = scr("xnext" + sfx, [TL, D], F32)
            phase_B(k, ioB, "b%d_" % l)
            xcur = ioB["y"]
    return nc


def kernel(x, w_in, b_in, conv_w, conv_b, conv_ln_g, conv_ln_b, w_pw2,
           w_proj_a, w_proj_b, w_out, ln_g, ln_b):
    x = np.asarray(x, dtype=np.float32)
    f = lambda a: np.asarray(a, dtype=np.float32)
    shared = {}
    for l in range(DEPTH):
        w_in_r, bias_col, b_v = prep_A_weights(f(w_in)[l], f(b_in)[l])
        shared["w_in_r%d" % l] = w_in_r
        shared["bias_col%d" % l] = bias_col
        shared["b_v%d" % l] = b_v
        wB = prep_B_weights(f(conv_w)[l], f(conv_b)[l], f(conv_ln_g)[l], f(conv_ln_b)[l], f(w_pw2)[l],
                            f(w_proj_a)[l], f(w_proj_b)[l], f(w_out)[l], f(ln_g)[l], f(ln_b)[l])
        for n, v in wB.items():
            shared["%s%d" % (n, l)] = v
    shared.update(shared_consts())
    shared["ident_f"] = np.eye(128, dtype=np.float32)
    in_maps = []
    for c in range(8):
        b, j = c // 4, c % 4
        m = dict(shared)
        m["x"] = np.ascontiguousarray(x[b][core_tokens(j)])
        cos, sins = rope_tables(j)
        m["cos"] = cos
        m["sins"] = sins
        m.update(core_consts(j))
        in_maps.append(m)
    nc = build_fused()
    res = run_bass_kernel_spmd(nc, in_maps, core_ids=list(range(8)))
    out = np.zeros((B, S, D), np.float32)
    for c in range(8):
        b, j = c // 4, c % 4
        out[b][core_tokens(j)] = np.asarray(res.results[c]["y"])
    return out
```

```python
import math
from contextlib import ExitStack

import numpy as np
import ml_dtypes

import concourse.bass as bass
import concourse.mybir as mybir
from concourse.bass_utils import run_bass_kernel_spmd

F32 = mybir.dt.float32
BF16 = mybir.dt.bfloat16
AF = mybir.ActivationFunctionType
ALU = mybir.AluOpType
AX = mybir.AxisListType

D = 1024
S = 8192
B = 2
DEPTH = 2
NH = 16
HD = 64
BLK = 256
NBLK = S // BLK
NLB = 8
TL = NLB * BLK
CK = 31
HALO = 32
TOPK = 3
LN_EPS = 1e-5
ALPHA = (2 * DEPTH) ** 0.25
NEGBIG = -32768.0
SAME_SYNC = True

OFF = dict(val=0, glu=1024, az=2048, q=3072, k=4096, v=5120, bz=6144, ga=7168, gb=8192)


class Buf:
    __slots__ = ("name", "w", "r")

    def __init__(self, name=""):
        self.name = name
        self.w = None
        self.r = {}


class K:
    def __init__(self, nc, st):
        self.nc = nc
        self.st = st
        self.E = {"pe": nc.tensor, "act": nc.scalar, "dve": nc.vector,
                  "pool": nc.gpsimd, "sp": nc.sync}
        self.sem = {n: st.enter_context(nc.semaphore("prog_" + n)) for n in self.E}
        self.cnt = {n: 0 for n in self.E}
        self.seen = {n: {} for n in self.E}
        self.dring = {}
        self.dpos = {}
        for q, n in (("sp", 32), ("pool", 16), ("act", 24)):
            self.dring[q] = [[st.enter_context(nc.semaphore("d_%s%d" % (q, i))), 0, None]
                             for i in range(n)]
            self.dpos[q] = 0
        self.n_inst = 0

    def sb(self, name, shape, dt):
        return self.st.enter_context(self.nc.sbuf_tensor(name, list(shape), dt))

    def ps(self, name, shape, dt=F32):
        return self.st.enter_context(self.nc.psum_tensor(name, list(shape), dt))

    def _wait(self, en, toks):
        need = {}
        for tok, kind in toks:
            if tok is None:
                continue
            sem, key, val, src = tok
            if src == en and not (kind == "raw" and SAME_SYNC):
                continue
            if self.seen[en].get(key, 0) >= val:
                continue
            if key not in need or need[key][1] < val:
                need[key] = (sem, val)
        for key, (sem, val) in need.items():
            self.E[en].wait_ge(sem, val)
            self.seen[en][key] = val

    @staticmethod
    def _deps(reads, writes):
        toks = []
        for b in reads:
            if b.w is not None:
                toks.append((b.w, "raw"))
        for b in writes:
            if b.w is not None:
                toks.append((b.w, "waw"))
            for t in b.r.values():
                toks.append((t, "war"))
        return toks

    @staticmethod
    def _reg(tok, reads, writes):
        key = tok[1]
        for b in reads:
            b.r[key] = tok
        for b in writes:
            b.w = tok
            b.r = {}

    def op(self, en, emit, reads=(), writes=()):
        self._wait(en, self._deps(reads, writes))
        inst = emit(self.E[en])
        self.cnt[en] += 1
        inst.then_inc(self.sem[en], 1)
        tok = (self.sem[en], en, self.cnt[en], en)
        self._reg(tok, reads, writes)
        self.n_inst += 1
        return tok

    def dma(self, q, out, in_, reads=(), writes=()):
        ring = self.dring[q]
        slot = ring[self.dpos[q] % len(ring)]
        self.dpos[q] += 1
        toks = self._deps(reads, writes)
        if slot[2] is not None:
            toks.append((slot[2], "raw"))
        self._wait(q, toks)
        inst = self.E[q].dma_start(out=out, in_=in_)
        slot[1] += 16
        inst.then_inc(slot[0], 16)
        tok = (slot[0], ("d", q, id(slot)), slot[1], None)
        slot[2] = tok
        self._reg(tok, reads, writes)
        self.n_inst += 1
        return tok

    def wait_all(self, en, bufs):
        toks = []
        for b in bufs:
            if b.w is not None:
                toks.append((b.w, "raw"))
            for t in b.r.values():
                toks.append((t, "raw"))
        self._wait(en, toks)

    def barrier(self, bufs_all=None):
        toks = []
        for n in ("pe", "act", "dve", "pool"):
            if self.cnt[n] > 0:
                toks.append(((self.sem[n], n, self.cnt[n], None), "raw"))
        for q in self.dring:
            for slot in self.dring[q]:
                if slot[2] is not None:
                    toks.append((slot[2], "raw"))
        for en in self.E:
            self._wait(en, toks)


def v3(ap, a):
    return ap.rearrange("p (a b) -> p a b", a=a)


A_ORDER = [("v", _c) for _c in range(8)]
for _c in range(8):
    A_ORDER += [("kp", _c), ("k", _c)]
KV_DONE_AT = len(A_ORDER)
for _c in range(8):
    A_ORDER += [("glu", _c), ("val", _c)]
HOOK_AT = len(A_ORDER)
for _c in range(8):
    A_ORDER += [("qp", _c), ("q", _c)]
for _f in ("az", "bz", "ga", "gb"):
    for _c in range(8):
        A_ORDER.append((_f, _c))
N_WCH = len(A_ORDER)
N_ACH = N_WCH


def phase_A(k, io, pfx):
    nc = k.nc
    x, w_in_r, bias_col, b_v = io["x"], io["w_in_r"], io["bias_col"], io["b_v"]
    cos_d, sins_d, ident_d = io["cos"], io["sins"], io["ident_f"]
    with ExitStack() as st:
        sb = lambda n, s, d: st.enter_context(nc.sbuf_tensor(pfx + n, list(s), d))
        xT = sb("xT", [128, 8, TL], BF16)
        xTb = [Buf("xT%d" % t) for t in range(16)]
        xin = [sb("xin%d" % i, [128, D], F32) for i in range(4)]
        xinb = [Buf() for _ in range(4)]
        wf = [sb("wf%d" % i, [128, D], F32) for i in range(4)]
        wfb = [Buf() for _ in range(4)]
        wb = [sb("wb%d" % i, [128, 8, 128], BF16) for i in range(2)]
        wbb = [Buf() for _ in range(2)]
        T1 = [sb("T1_%d" % i, [128, TL], F32) for i in range(2)]
        T1b = [Buf() for _ in range(2)]
        T2 = [sb("T2_%d" % i, [128, TL], F32) for i in range(2)]
        T2b = [Buf() for _ in range(2)]
        Kf = sb("Kf", [128, TL], F32)
        Kfb = Buf()
        SG = sb("SG", [128, TL], F32)
        SGb = Buf()
        OB = [sb("OB%d" % i, [128, TL], BF16) for i in range(3)]
        OBb = [Buf() for _ in range(3)]
        cos_t = sb("cos", [128, TL], F32)
        sin_t = sb("sin", [128, TL], F32)
        tabb = Buf()
        wv = sb("wv", [128, 8, D], BF16)
        wvb = Buf()
        Vst = sb("Vst", [128, NH, 16, 65], BF16)
        Vstb = Buf()
        bias_t = sb("bias", [128, N_ACH], F32)
        biasb = Buf()
        ident = sb("ident", [128, 128], F32)
        identb = Buf()
        bvf = sb("bvf", [1, D], F32)
        bvb16 = sb("bvb16", [1, D], BF16)
        bvb = Buf()
        ones1 = sb("ones1", [1, 128], BF16)
        ones1b = Buf()
        km = [sb("km%d" % i, [128, 8], F32) for i in range(2)]
        km2 = [sb("kmm%d" % i, [128, 8], F32) for i in range(2)]
        kmb = [Buf() for _ in range(2)]
        km2b = [Buf() for _ in range(2)]
        ps = [st.enter_context(nc.psum_tensor(pfx + "ps%d" % i, [128, 512], F32)) for i in range(8)]
        psb = [Buf("ps%d" % i) for i in range(8)]

        k.dma("sp", ident[:, :], ident_d[:, :], writes=[identb])
        k.dma("sp", bias_t[:, :], bias_col[:, :], writes=[biasb])
        k.dma("sp", cos_t[:, :], cos_d[:, :], writes=[tabb])
        k.dma("sp", sin_t[:, :], sins_d[:, :], writes=[tabb])
        k.dma("sp", bvf[:, :], b_v[:, :], writes=[bvb])
        k.op("pool", lambda e: e.tensor_copy(out=bvb16[:, :], in_=bvf[:, :]), reads=[bvb], writes=[bvb])
        k.op("pool", lambda e: e.memset(ones1[:, :], 1.0), writes=[ones1b])
        k.op("pool", lambda e: e.memset(Vst[:, :, :, :].rearrange("p a b c -> p (a b c)"), 1.0), writes=[Vstb])

        def load_chunk(ci, i):
            s = i % 4
            k.dma("sp", wf[s][:, :], w_in_r[ci], writes=[wfb[s]])
        load_chunk(0, 0)
        load_chunk(1, 1)

        ev = 0
        for tt in range(16):
            s = tt % 4
            k.dma("sp", xin[s][:, :], x[tt * 128:(tt + 1) * 128, :], writes=[xinb[s]])
            for half in range(2):
                bk = (tt * 2 + half) % 8

                def emit(e, s=s, half=half, bk=bk):
                    last = None
                    for j in range(4):
                        kk = half * 4 + j
                        last = e.transpose(out=ps[bk][:, j * 128:(j + 1) * 128],
                                           in_=xin[s][:, kk * 128:(kk + 1) * 128],
                                           identity=ident[:, :])
                    return last
                k.op("pe", emit, reads=[xinb[s], identb], writes=[psb[bk]])
                dst = xT[:, half * 4:(half + 1) * 4, tt * 128:(tt + 1) * 128]
                src = v3(ps[bk][:, :], 4)
                if ev % 2 == 0:
                    k.op("act", lambda e, dst=dst, src=src: e.activation(out=dst, in_=src, func=AF.Copy),
                         reads=[psb[bk]], writes=[xTb[tt]])
                else:
                    k.op("dve", lambda e, dst=dst, src=src: e.tensor_copy(out=dst, in_=src),
                         reads=[psb[bk]], writes=[xTb[tt]])
                ev += 1


        ob_i = [0]

        def next_ob():
            i = ob_i[0] % 3
            ob_i[0] += 1
            return i

        toks = []
        toks_h = []
        def do_V():
            for tt in range(16):
                for half in range(2):
                    bk = (tt * 2 + half) % 8

                    def emit(e, tt=tt, half=half, bk=bk):
                        for kk in range(8):
                            e.matmul(out=ps[bk][:, :], lhsT=xT[:, kk, tt * 128:(tt + 1) * 128],
                                     rhs=wv[:, kk, half * 512:(half + 1) * 512], start=(kk == 0), stop=False)
                        return e.matmul(out=ps[bk][:, :], lhsT=ones1[0:1, :], rhs=bvb16[0:1, half * 512:(half + 1) * 512],
                                        start=False, stop=True)
                    k.op("pe", emit, reads=[xTb[tt], wvb, ones1b, bvb], writes=[psb[bk]])
                    dst = Vst[:, half * 8:(half + 1) * 8, tt, 0:HD]
                    src = v3(ps[bk][:, :], 8)
                    if (tt + half) % 2 == 0:
                        k.op("act", lambda e, dst=dst, src=src: e.activation(out=dst, in_=src, func=AF.Copy),
                             reads=[psb[bk]], writes=[Vstb])
                    else:
                        k.op("dve", lambda e, dst=dst, src=src: e.tensor_copy(out=dst, in_=src),
                             reads=[psb[bk]], writes=[Vstb])
            for pc in range(8):
                toks.append(k.dma("pool", io["Vp_piece"](pc), Vst[:, 2 * pc:2 * pc + 2, :, :].rearrange("p a b c -> p (a b c)"), reads=[Vstb]))

        def cast_chunk(i):
            kind, c = A_ORDER[i]
            s = i % 2
            f = i % 4
            if kind == "v":
                k.op("act", lambda e: e.activation(out=wv[:, :, c * 128:(c + 1) * 128], in_=v3(wf[f][:, :], 8), func=AF.Copy),
                     reads=[wfb[f]], writes=[wvb])
            else:
                k.op("act", lambda e: e.activation(out=wb[s][:, :, :].rearrange("p a b -> p (a b)"), in_=wf[f][:, :], func=AF.Copy),
                     reads=[wfb[f]], writes=[wbb[s]])

        load_chunk(2, 2)
        load_chunk(3, 3)
        cast_chunk(0)
        for i, (kind, c) in enumerate(A_ORDER):
            s = i % 2
            if i == KV_DONE_AT + 4 and io.get("hook_kv") is not None:
                io["hook_kv"](toks)
            if i == HOOK_AT + 4 and io.get("hook") is not None:
                io["hook"](toks_h)
            if i + 1 < N_WCH:
                cast_chunk(i + 1)
            if i + 4 < N_WCH:
                load_chunk(i + 4, i + 4)
            if kind == "v":
                if c == 7:
                    do_V()
                continue
            banks = [(i % 2) * 4 + g for g in range(4)]
            for g in range(4):
                bk = banks[g]

                def emit(e, s=s, g=g, bk=bk):
                    last = None
                    for kk in range(8):
                        last = e.matmul(out=ps[bk][:, :], lhsT=wb[s][:, kk, :],
                                        rhs=xT[:, kk, g * 512:(g + 1) * 512],
                                        start=(kk == 0), stop=(kk == 7))
                    return last
                k.op("pe", emit, reads=[wbb[s]] + xTb[4 * g:4 * g + 4], writes=[psb[bk]])
            bias = bias_t[:, i:i + 1]
            t = (i // 2) % 2
            if kind in ("qp", "kp"):
                for g in range(4):
                    bk = banks[g]
                    k.op("dve", lambda e, bk=bk, g=g, t=t, bias=bias: e.scalar_tensor_tensor(
                        out=T2[t][:, g * 512:(g + 1) * 512], in0=ps[bk][:, :], scalar=bias,
                        in1=sin_t[:, g * 512:(g + 1) * 512], op0=ALU.add, op1=ALU.mult),
                        reads=[psb[bk], biasb, tabb], writes=[T2b[t]])
            elif kind in ("q", "k"):
                for g in range(4):
                    bk = banks[g]
                    k.op("dve", lambda e, bk=bk, g=g, t=t, bias=bias: e.scalar_tensor_tensor(
                        out=T1[t][:, g * 512:(g + 1) * 512], in0=ps[bk][:, :], scalar=bias,
                        in1=cos_t[:, g * 512:(g + 1) * 512], op0=ALU.add, op1=ALU.mult),
                        reads=[psb[bk], biasb, tabb], writes=[T1b[t]])
                o = next_ob()
                if kind == "q":
                    k.op("pool", lambda e, t=t: e.tensor_tensor(out=T1[t][:, :], in0=T1[t][:, :], in1=T2[t][:, :], op=ALU.add),
                         reads=[T1b[t], T2b[t]], writes=[T1b[t]])
                    k.op("act", lambda e, o=o, t=t: e.activation(out=OB[o][:, :], in_=T1[t][:, :], func=AF.Copy),
                         reads=[T1b[t]], writes=[OBb[o]])
                    k.dma("act", io["qT"][c * 128:(c + 1) * 128, :], OB[o][:, :], reads=[OBb[o]])
                    o2 = next_ob()
                    k.op("dve", lambda e, t=t, o=o, o2=o2: e.tensor_tensor(out=OB[o2][:, :], in0=T1[t][:, :], in1=OB[o][:, :], op=ALU.subtract),
                         reads=[T1b[t], OBb[o]], writes=[OBb[o2]])
                    k.dma("sp", io["qlo"][c * 128:(c + 1) * 128, :], OB[o2][:, :], reads=[OBb[o2]])
                else:
                    k.op("pool", lambda e, t=t: e.tensor_tensor(out=Kf[:, :], in0=T1[t][:, :], in1=T2[t][:, :], op=ALU.add),
                         reads=[T1b[t], T2b[t]], writes=[Kfb])
                    k.op("act", lambda e, o=o: e.activation(out=OB[o][:, :], in_=Kf[:, :], func=AF.Copy),
                         reads=[Kfb], writes=[OBb[o]])
                    toks.append(k.dma("act", io["KT_rows"](c), OB[o][:, :], reads=[OBb[o]]))
                    m_ = c % 2
                    k.op("dve", lambda e, m_=m_: e.tensor_reduce(out=km[m_][:, :], in_=v3(Kf[:, :], 8), axis=AX.X, op=ALU.add),
                         reads=[Kfb], writes=[kmb[m_]])
                    k.op("dve", lambda e, m_=m_: e.tensor_scalar(out=km2[m_][:, :], in0=km[m_][:, :], scalar1=1.0 / BLK, scalar2=None, op0=ALU.mult),
                         reads=[kmb[m_]], writes=[km2b[m_]])
                    toks.append(k.dma("act", io["kmean"][c * 128:(c + 1) * 128, :], km2[m_][:, :], reads=[km2b[m_]]))
            elif kind == "glu":
                for g in range(4):
                    bk = banks[g]
                    k.op("act", lambda e, bk=bk, g=g, bias=bias: e.activation(
                        out=SG[:, g * 512:(g + 1) * 512], in_=ps[bk][:, :], func=AF.Sigmoid, bias=bias, scale=1.0),
                        reads=[psb[bk], biasb], writes=[SGb])
            elif kind == "val":
                o = next_ob()
                for g in range(4):
                    bk = banks[g]
                    k.op("dve", lambda e, bk=bk, g=g, o=o, bias=bias: e.scalar_tensor_tensor(
                        out=OB[o][:, g * 512:(g + 1) * 512], in0=ps[bk][:, :], scalar=bias,
                        in1=SG[:, g * 512:(g + 1) * 512], op0=ALU.add, op1=ALU.mult),
                        reads=[psb[bk], biasb, SGb], writes=[OBb[o]])
                k.dma("act", io["hT"][c * 128:(c + 1) * 128, :], OB[o][:, :], reads=[OBb[o]])
                toks_h.append(k.dma("act", v3(io["htail"][c * 128:(c + 1) * 128, :], 8),
                                  v3(OB[o][:, :], 8)[:, :, BLK - HALO:BLK], reads=[OBb[o]]))
            else:
                fn = AF.Silu if kind in ("az", "bz") else AF.Sigmoid
                dst = {"az": "saz", "bz": "sbz", "ga": "sga", "gb": "sgb"}[kind]
                o = next_ob()
                for g in range(4):
                    bk = banks[g]
                    k.op("act", lambda e, bk=bk, g=g, o=o, bias=bias, fn=fn: e.activation(
                        out=OB[o][:, g * 512:(g + 1) * 512], in_=ps[bk][:, :], func=fn, bias=bias, scale=1.0),
                        reads=[psb[bk], biasb], writes=[OBb[o]])
                k.dma("act", io[dst][c * 128:(c + 1) * 128, :], OB[o][:, :], reads=[OBb[o]])

        k.barrier()


def _bf(a):
    return np.ascontiguousarray(a).astype(ml_dtypes.bfloat16)


def core_tokens(j):
    lt = np.arange(TL)
    return (4 * (lt // BLK) + j) * BLK + (lt % BLK)


def prep_A_weights(w_in_l, b_in_l):
    cols = []
    for kind, c in A_ORDER:
        p = np.arange(128)
        if kind in ("qp", "kp"):
            base = OFF["q"] if kind == "qp" else OFF["k"]
            head = 2 * c + p // 64
            d = p % 64
            cols.append(base + head * 64 + (d + 32) % 64)
        else:
            cols.append(OFF[kind] + c * 128 + p)
    cols = np.stack(cols)
    w = w_in_l[:, cols]
    w = w.reshape(8, 128, N_WCH, 128).transpose(2, 1, 0, 3)
    w_in_r = np.ascontiguousarray(w.reshape(N_WCH, 128, 1024), dtype=np.float32)
    bias_col = np.ascontiguousarray(b_in_l[cols].T, dtype=np.float32)
    b_v = np.ascontiguousarray(b_in_l[OFF["v"]:OFF["v"] + D][None, :], dtype=np.float32)
    return w_in_r, bias_col, b_v


def rope_tables(j):
    pos = core_tokens(j).astype(np.float64)
    p = np.arange(128)
    d = p % 64
    f = d % 32
    inv = 10000.0 ** (-f.astype(np.float64) / 32.0)
    ang = inv[:, None] * pos[None, :]
    ang32 = (pos.astype(np.float32)[None, :] * (10000.0 ** (-(f.astype(np.float32)) / 32.0)).astype(np.float32)[:, None]).astype(np.float32)
    cos = np.cos(ang32.astype(np.float64)).astype(np.float32)
    sin = np.sin(ang32.astype(np.float64)).astype(np.float32)
    sign = np.where(d < 32, -1.0, 1.0).astype(np.float32)[:, None]
    return cos, sin * sign


def phase_B(k, io, pfx):
    nc = k.nc
    with ExitStack() as st0:
        sb0 = lambda n, s, d: st0.enter_context(nc.sbuf_tensor(pfx + n, list(s), d))
        ps = [st0.enter_context(nc.psum_tensor(pfx + "ps%d" % i, [128, 512], F32)) for i in range(8)]
        psb = [Buf("ps%d" % i) for i in range(8)]
        Ma = sb0("Ma", [128, 8, TL], BF16)
        Mab = [Buf("Ma%d" % g) for g in range(4)]
        identb_t = sb0("identb", [128, 128], BF16)
        negI = sb0("negI", [128, 128], BF16)
        onesbb = sb0("onesbb", [128, 128], BF16)
        onesf = sb0("onesf", [128, 64], F32)
        cst = Buf("consts")
        wst = [sb0("wst%d" % i, [128, D], F32) for i in range(2)]
        wstb = [Buf() for _ in range(2)]
        wctr = [0]
        k.dma("sp", identb_t[:, :], io["ident_b"][:, :], writes=[cst])
        k.dma("sp", negI[:, :], io["negI"][:, :], writes=[cst])
        k.op("pool", lambda e: e.memset(onesbb[:, :], 1.0), writes=[cst])
        k.op("pool", lambda e: e.memset(onesf[:, :], 1.0), writes=[cst])

        def load_weight(wi, dst, dstb):
            for kk in range(8):
                s = wctr[0] % 2
                wctr[0] += 1
                k.dma("sp", wst[s][:, :], io["w4"][wi, :, kk, :], writes=[wstb[s]])
                if wi >= 2:
                    k.op("pool", lambda e, s=s, kk=kk: e.tensor_copy(out=dst[:, kk, :], in_=wst[s][:, :]),
                         reads=[wstb[s]], writes=[dstb])
                else:
                    k.op("act", lambda e, s=s, kk=kk: e.activation(out=dst[:, kk, :], in_=wst[s][:, :], func=AF.Copy),
                         reads=[wstb[s]], writes=[dstb])

        Kaug0 = sb0("Kaug0", [128, 4, TL], BF16)
        Qaug0 = sb0("Qaug0", [128, TL], BF16)
        Kb0, Qb0, Qab0 = Buf(), Buf(), Buf()

        def early_head0():
            if io.get("pre_attn") is not None:
                io["pre_attn"]()
            k.dma("sp", Kaug0[64:96, :, :].rearrange("p a b -> p (a b)"), io["ind"][:, :], writes=[Kb0])
            k.dma("sp", Qaug0[0:64, :], io["qT"][0:64, :], writes=[Qb0])
            for r in range(4):
                k.dma("sp", Kaug0[0:64, r, :], io["KT_all_head"](r, 0), writes=[Kb0])

        with ExitStack() as st1:
            sb1 = lambda n, s, d: st1.enter_context(nc.sbuf_tensor(pfx + n, list(s), d))
            Cbuf = sb1("Cbuf", [128, 8, TL], BF16)
            Cb = [[Buf() for _ in range(4)] for _ in range(8)]
            cw_t = sb1("cw", [128, 8, CK], F32)
            cvec = sb1("cvec", [128, 3, 8], F32)
            cvb = Buf()
            k.dma("sp", cw_t[:, :, :], io["cw"][:, :, :], writes=[cvb])
            k.dma("sp", cvec[:, :, :], io["cvec"][:, :, :], writes=[cvb])
            wpw2 = sb1("wpw2", [128, 8, D], BF16)
            wpa = sb1("wpa", [128, 8, D], BF16)
            wpw2b, wpab = Buf(), Buf()
            with ExitStack() as st2:
                sb2 = lambda n, s, d: st2.enter_context(nc.sbuf_tensor(pfx + n, list(s), d))
                hbuf = sb2("hbuf", [128, 8, NLB, HALO + BLK], BF16)
                hb = [Buf() for _ in range(8)]
                halob = Buf()
                cand = sb2("cand", [128, 4, 8, NLB, HALO], BF16)
                candb = Buf()
                acc = sb2("acc", [128, 8 * NLB * HALO], F32)
                accb = Buf()
                sel4 = sb2("sel4", [128, 4], F32)
                diag = [sb2("diag%d" % i, [128, CK, 128], BF16) for i in range(2)]
                diagb = [Buf() for _ in range(2)]
                diagb2 = [Buf() for _ in range(2)]
                k.dma("sp", sel4[:, :], io["sel4"][:, :], writes=[candb])
                k.op("pool", lambda e: e.memset(cand[:, 3, :, :, :].rearrange("p a b c -> p (a b c)"), 0.0), writes=[candb])
                for r in range(3):
                    k.dma("sp", cand[:, r, :, :, :].rearrange("p a b c -> p a (b c)"),
                          io["htail_all"][r * D:(r + 1) * D, :].rearrange("(c p) f -> p c f", p=128), writes=[candb])
                k.dma("sp", cand[:, 3, :, 1:NLB, :],
                      io["htail_all"][3 * D:4 * D, :].rearrange("(c p) (m f) -> p c m f", p=128, f=HALO)[:, :, 0:NLB - 1, :],
                      writes=[candb])
                for c in range(8):
                    k.dma("sp", hbuf[:, c, :, HALO:HALO + BLK],
                          io["hT"][c * 128:(c + 1) * 128, :].rearrange("p (m t) -> p m t", m=NLB), writes=[hb[c]])
                if io.get("kv_gather") is not None:
                    io["kv_gather"]()
                cflat = lambda r: cand[:, r, :, :, :].rearrange("p a b c -> p (a b c)")
                k.op("dve", lambda e: e.tensor_scalar(out=acc[:, :], in0=cflat(0), scalar1=sel4[:, 0:1], scalar2=None, op0=ALU.mult),
                     reads=[candb], writes=[accb])
                for r in range(1, 4):
                    k.op("dve", lambda e, r=r: e.scalar_tensor_tensor(out=acc[:, :], in0=cflat(r), scalar=sel4[:, r:r + 1],
                                                                     in1=acc[:, :], op0=ALU.mult, op1=ALU.add),
                         reads=[candb, accb], writes=[accb])
                k.op("dve", lambda e: e.tensor_copy(
                    out=hbuf[:, :, :, 0:HALO].rearrange("p a b c -> p (a b) c"),
                    in_=acc[:, :].rearrange("p (a c) -> p a c", c=HALO)),
                    reads=[accb], writes=[halob])
                for c in range(8):
                    s = c % 2
                    if c == 2:
                        load_weight(0, wpw2, wpw2b)
                    if c == 4:
                        load_weight(1, wpa, wpab)
                    for j in range(CK):
                        if j % 2 == 0:
                            k.op("act", lambda e, s=s, c=c, j=j: e.activation(
                                out=diag[s][:, j, :], in_=identb_t[:, :], func=AF.Copy, scale=cw_t[:, c, j:j + 1]),
                                reads=[cst, cvb], writes=[diagb[s]])
                        else:
                            k.op("dve", lambda e, s=s, c=c, j=j: e.tensor_scalar(
                                out=diag[s][:, j, :], in0=identb_t[:, :], scalar1=cw_t[:, c, j:j + 1], scalar2=None, op0=ALU.mult),
                                reads=[cst, cvb], writes=[diagb2[s]])
                    for pr in range(4):
                        bk = (c * 4 + pr) % 8

                        def emit(e, s=s, c=c, pr=pr, bk=bk):
                            last = None
                            for j in range(CK):
                                last = e.matmul(out=ps[bk][:, :], lhsT=diag[s][:, j, :],
                                                rhs=hbuf[:, c, 2 * pr:2 * pr + 2, j + 2:j + 2 + BLK],
                                                start=(j == 0), stop=(j == CK - 1))
                            return last
                        k.op("pe", emit, reads=[diagb[s], diagb2[s], hb[c], halob], writes=[psb[bk]])
                        k.op("dve", lambda e, c=c, pr=pr, bk=bk: e.tensor_scalar(
                            out=Cbuf[:, c, pr * 512:(pr + 1) * 512], in0=ps[bk][:, :], scalar1=cvec[:, 0, c:c + 1],
                            scalar2=None, op0=ALU.add), reads=[psb[bk], cvb], writes=[Cb[c][pr]])
                k.barrier()
            with ExitStack() as st2:
                sb2 = lambda n, s, d: st2.enter_context(nc.sbuf_tensor(pfx + n, list(s), d))
                sq = [sb2("sq%d" % i, [128, 512], BF16) for i in range(2)]
                sqb = [Buf() for _ in range(2)]
                mu = [sb2("mu%d" % i, [128, 512], F32) for i in range(2)]
                msq = sb2("msq", [128, 512], F32)
                var = sb2("var", [128, 512], F32)
                rstd = [sb2("rstd%d" % i, [128, 512], F32) for i in range(2)]
                mub = [Buf() for _ in range(2)]
                rstdb = [Buf() for _ in range(2)]
                msqb, varb = Buf(), Buf()
                tmp = [sb2("tmp%d" % i, [128, 512], F32) for i in range(2)]
                tmpb = [Buf() for _ in range(2)]
                tmq = [sb2("tmq%d" % i, [128, 512], F32) for i in range(2)]
                tmqb = [Buf() for _ in range(2)]
                Nbuf = [sb2("Nbuf%d" % i, [128, 8, 512], BF16) for i in range(2)]
                Nb = [[Buf() for _ in range(8)] for _ in range(2)]
                Gbuf = sb2("Gbuf", [128, 8, 512], BF16)
                Gb = [Buf() for _ in range(8)]
                sazg = [sb2("sazg%d" % i, [128, 8, 512], BF16) for i in range(2)]
                sgag = [sb2("sgag%d" % i, [128, 8, 512], BF16) for i in range(2)]
                sazb = [Buf() for _ in range(2)]
                sgab = [Buf() for _ in range(2)]
                epsb_t = sb2("eps", [128, 1], F32)
                k.op("pool", lambda e: e.memset(epsb_t[:, :], LN_EPS), writes=[cst])
                GS = [slice(g * 512, (g + 1) * 512) for g in range(4)]
                early_head0()

                def stage_S(g):
                    gs = GS[g]
                    q2 = g % 2
                    k.dma("sp", sazg[q2][:, :, :], io["saz"].rearrange("(c p) t -> p c t", p=128)[:, :, gs], writes=[sazb[q2]])
                    k.dma("sp", sgag[q2][:, :, :], io["sga"].rearrange("(c p) t -> p c t", p=128)[:, :, gs], writes=[sgab[q2]])

                    def emit(e):
                        last = None
                        for c in range(8):
                            last = e.matmul(out=ps[0][:, :], lhsT=onesbb[:, :], rhs=Cbuf[:, c, gs], start=(c == 0), stop=(c == 7))
                        return last
                    k.op("pe", emit, reads=[cst] + [Cb[c][g] for c in range(8)], writes=[psb[0]])
                    for c in range(8):
                        s = c % 2
                        k.op("act", lambda e, s=s, c=c: e.activation(out=sq[s][:, :], in_=Cbuf[:, c, gs], func=AF.Square),
                             reads=[Cb[c][g]], writes=[sqb[s]])
                        k.op("pe", lambda e, s=s, c=c: e.matmul(out=ps[1][:, :], lhsT=onesbb[:, :], rhs=sq[s][:, :],
                                                              start=(c == 0), stop=(c == 7)),
                             reads=[cst, sqb[s]], writes=[psb[1]])
                    k.op("dve", lambda e: e.tensor_scalar(out=mu[q2][:, :], in0=ps[0][:, :], scalar1=1.0 / D, scalar2=None, op0=ALU.mult),
                         reads=[psb[0]], writes=[mub[q2]])
                    k.op("dve", lambda e: e.tensor_tensor(out=msq[:, :], in0=mu[q2][:, :], in1=mu[q2][:, :], op=ALU.mult),
                         reads=[mub[q2]], writes=[msqb])
                    k.op("dve", lambda e: e.scalar_tensor_tensor(out=var[:, :], in0=ps[1][:, :], scalar=1.0 / D, in1=msq[:, :],
                                                                 op0=ALU.mult, op1=ALU.subtract),
                         reads=[psb[1], msqb], writes=[varb])
                    k.op("act", lambda e: e.activation(out=var[:, :], in_=var[:, :], func=AF.Sqrt, bias=epsb_t[:, 0:1], scale=1.0),
                         reads=[varb, cst], writes=[varb])
                    k.op("dve", lambda e: e.reciprocal(out=rstd[q2][:, :], in_=var[:, :]), reads=[varb], writes=[rstdb[q2]])

                def stage_N(g, c):
                    gs = GS[g]
                    q2 = g % 2
                    s = c % 2
                    k.op("dve", lambda e: e.tensor_tensor(out=tmp[s][:, :], in0=Cbuf[:, c, gs], in1=mu[q2][:, :], op=ALU.subtract),
                         reads=[Cb[c][g], mub[q2]], writes=[tmpb[s]])
                    k.op("pool", lambda e: e.tensor_tensor(out=tmq[s][:, :], in0=tmp[s][:, :], in1=rstd[q2][:, :], op=ALU.mult),
                         reads=[tmpb[s], rstdb[q2]], writes=[tmqb[s]])
                    k.op("act", lambda e: e.activation(out=Nbuf[q2][:, c, :], in_=tmq[s][:, :], func=AF.Silu,
                                                       bias=cvec[:, 2, c:c + 1], scale=cvec[:, 1, c:c + 1]),
                         reads=[tmqb[s], cvb], writes=[Nb[q2][c]])

                def stage_P(g, c2):
                    q2 = g % 2
                    bk = 2 + (c2 % 3)

                    def emit(e):
                        last = None
                        for kk in range(8):
                            last = e.matmul(out=ps[bk][:, :], lhsT=wpw2[:, kk, c2 * 128:(c2 + 1) * 128], rhs=Nbuf[q2][:, kk, :],
                                            start=(kk == 0), stop=(kk == 7))
                        return last
                    k.op("pe", emit, reads=[wpw2b] + Nb[q2], writes=[psb[bk]])
                    k.op("dve", lambda e: e.tensor_tensor(out=Gbuf[:, c2, :], in0=ps[bk][:, :], in1=sazg[q2][:, c2, :], op=ALU.mult),
                         reads=[psb[bk], sazb[q2]], writes=[Gb[c2]])

                def stage_A(g, c2):
                    gs = GS[g]
                    q2 = g % 2
                    bk = 5 + (c2 % 3)

                    def emit(e):
                        last = None
                        for kk in range(8):
                            last = e.matmul(out=ps[bk][:, :], lhsT=wpa[:, kk, c2 * 128:(c2 + 1) * 128], rhs=Gbuf[:, kk, :],
                                            start=(kk == 0), stop=(kk == 7))
                        return last
                    k.op("pe", emit, reads=[wpab] + Gb, writes=[psb[bk]])
                    k.op("dve", lambda e: e.tensor_tensor(out=Ma[:, c2, gs], in0=ps[bk][:, :], in1=sgag[q2][:, c2, :], op=ALU.mult),
                         reads=[psb[bk], sgab[q2]], writes=[Mab[g]])

                stage_S(0)
                for c in range(8):
                    stage_N(0, c)
                for g in range(4):
                    if g + 1 < 4:
                        stage_S(g + 1)
                    for c2 in range(8):
                        stage_P(g, c2)
                        if g + 1 < 4:
                            stage_N(g + 1, c2)
                    for c2 in range(8):
                        stage_A(g, c2)
                k.barrier()

        oT = io["oT"]
        oTb = Buf("oT")
        stw = st0.enter_context(ExitStack())
        wpb = stw.enter_context(nc.sbuf_tensor(pfx + "wpb", [128, 8, D], BF16))
        wo = stw.enter_context(nc.sbuf_tensor(pfx + "wo", [128, 8, D], BF16))
        wpbb, wob = Buf(), Buf()
        with ExitStack() as st1:
            sb1 = lambda n, s, d: st1.enter_context(nc.sbuf_tensor(pfx + n, list(s), d))
            Kaug = [Kaug0, sb1("Kaug1", [128, 4, TL], BF16)]
            Vaug = [sb1("Vaug%d" % i, [128, 4, 16, 65], BF16) for i in range(2)]
            Qaug = [Qaug0, sb1("Qaug1", [128, TL], BF16)]
            Kb = [Kb0, Buf()]
            Vb = [Buf() for _ in range(2)]
            Qb = [Qb0, Buf()]
            Qab = [Qab0, Buf()]
            qlo = [sb1("qlo%d" % i, [64, TL], BF16) for i in range(2)]
            qlob = [Buf() for _ in range(2)]
            kml = [sb1("kml%d" % i, [64, 32], BF16) for i in range(2)]
            kmt = [sb1("kmt%d" % i, [64, 32], F32) for i in range(2)]
            kmlb = [Buf() for _ in range(2)]
            kmf = [sb1("kmf%d" % i, [64, 32], F32) for i in range(2)]
            kmbf = [sb1("kmbf%d" % i, [64, 32], BF16) for i in range(2)]
            kmfb = [Buf() for _ in range(2)]
            kmbb = [Buf() for _ in range(2)]
            pastb = sb1("pastb", [128, 512], F32)
            notown = sb1("notown", [128, 512], F32)
            masks = sb1("masks", [128, 4, 512], BF16)
            g0 = sb1("g0", [128, 512], F32)
            g1 = sb1("g1", [128, 512], F32)
            g2 = sb1("g2", [128, 512], F32)
            eq = sb1("eq", [128, 512], F32)
            mx = sb1("mx", [128, 16], F32)
            g0b, g1b, g2b, eqb, mxb = Buf(), Buf(), Buf(), Buf(), Buf()
            b01 = sb1("b01", [128, 16, 96], BF16)
            b01b = Buf()
            Pt = [sb1("Pt%d" % i, [128, 512], BF16) for i in range(16)]
            Ptb = [Buf() for _ in range(16)]
            rs = sb1("rs", [128, 512], F32)
            osb = [sb1("osb%d" % i, [64, 512], F32) for i in range(2)]
            rsb = Buf()
            osbb = [Buf() for _ in range(2)]
            ohead = [sb1("ohead%d" % i, [64, TL], BF16) for i in range(2)]
            oheadb = [Buf() for _ in range(2)]
            if io.get("pre_attn") is not None:
                io["pre_attn"]()
            k.dma("sp", pastb[:, :], io["pastb"][:, :], writes=[cst])
            k.dma("sp", notown[:, :], io["notown"][:, :], writes=[cst])
            k.dma("sp", masks[:, :, :], io["masks"][:, :, :], writes=[cst])
            k.op("pool", lambda e: e.memset(b01[:, :, :].rearrange("p a b -> p (a b)"), 0.0), writes=[b01b])
            k.dma("sp", Kaug[1][64:96, :, :].rearrange("p a b -> p (a b)"), io["ind"][:, :], writes=[Kb[1]])
            SB = [0, 1, 2, 3, 4]
            OBK = [5, 6]
            GBK, TBK, BCK = 7, 7, 7

            def prep1(h):
                s = h % 2
                k.dma("sp", qlo[s][:, :], io["qlo"][h * 64:(h + 1) * 64, :], writes=[qlob[s]])
                if h > 0:
                    k.dma("sp", Qaug[s][0:64, :], io["qT"][h * 64:(h + 1) * 64, :], writes=[Qb[s]])
                    for r in range(4):
                        k.dma("sp", Kaug[s][0:64, r, :], io["KT_all_head"](r, h), writes=[Kb[s]])
                k.dma("sp", Vaug[s][:, :, :, :].rearrange("p a b c -> p a (b c)"), io["Vp_all_head"](h), writes=[Vb[s]])
                k.dma("sp", kmf[s][:, :].rearrange("p (r m) -> p r m", r=4),
                      io["km_all"].rearrange("(r f) m -> f r m", r=4)[h * 64:(h + 1) * 64, :, :], writes=[kmfb[s]])
                k.op("dve", lambda e: e.tensor_copy(out=kmbf[s][:, :], in_=kmf[s][:, :]), reads=[kmfb[s]], writes=[kmbb[s]])
                k.op("dve", lambda e: e.tensor_tensor(out=kmt[s][:, :], in0=kmf[s][:, :], in1=kmbf[s][:, :], op=ALU.subtract),
                     reads=[kmfb[s], kmbb[s]], writes=[kmlb[s]])
                k.op("dve", lambda e: e.tensor_copy(out=kml[s][:, :], in_=kmt[s][:, :]), reads=[kmlb[s]], writes=[kmlb[s]])

            def prep1b(h):
                s = h % 2

                def emit(e):
                    last = None
                    for tt in range(16):
                        o_ = ps[GBK][:, tt * 32:(tt + 1) * 32]
                        ts_ = slice(tt * 128, (tt + 1) * 128)
                        e.matmul(out=o_, lhsT=Qaug[s][0:64, ts_], rhs=kmbf[s][:, :], start=True, stop=False)
                        e.matmul(out=o_, lhsT=qlo[s][0:64, ts_], rhs=kmbf[s][:, :], start=False, stop=False)
                        last = e.matmul(out=o_, lhsT=Qaug[s][0:64, ts_], rhs=kml[s][:, :], start=False, stop=True)
                    return last
                k.op("pe", emit, reads=[Qb[s], qlob[s], kmbb[s], kmlb[s]], writes=[psb[GBK]])
                V3 = lambda t: t[:, :].rearrange("p (a b) -> p a b", a=16)
                bc = lambda: mx[:, :].unsqueeze(2).broadcast_to([128, 16, 32])
                k.op("dve", lambda e: e.tensor_tensor(out=g0[:, :], in0=ps[GBK][:, :], in1=pastb[:, :], op=ALU.add),
                     reads=[psb[GBK], cst], writes=[g0b])
                src, srcb = g0, g0b
                for (dst, dstb) in ((g1, g1b), (g2, g2b)):
                    k.op("dve", lambda e, src=src: e.tensor_reduce(out=mx[:, :], in_=V3(src), axis=AX.X, op=ALU.max),
                         reads=[srcb], writes=[mxb])
                    k.op("dve", lambda e, src=src: e.tensor_tensor(out=V3(eq), in0=V3(src), in1=bc(), op=ALU.is_ge),
                         reads=[srcb, mxb], writes=[eqb])
                    k.op("dve", lambda e, src=src, dst=dst: e.scalar_tensor_tensor(out=dst[:, :], in0=eq[:, :], scalar=-1e30, in1=src[:, :],
                                                                                 op0=ALU.mult, op1=ALU.add),
                         reads=[eqb, srcb], writes=[dstb])
                    src, srcb = dst, dstb
                k.op("dve", lambda e: e.tensor_reduce(out=mx[:, :], in_=V3(g2), axis=AX.X, op=ALU.max), reads=[g2b], writes=[mxb])
                k.op("dve", lambda e: e.tensor_scalar(out=mx[:, :], in0=mx[:, :], scalar1=-1e29, scalar2=None, op0=ALU.max),
                     reads=[mxb], writes=[mxb])
                k.op("dve", lambda e: e.tensor_tensor(out=V3(eq), in0=V3(g0), in1=bc(), op=ALU.is_lt), reads=[g0b, mxb], writes=[eqb])
                k.op("dve", lambda e: e.tensor_tensor(out=b01[:, :, 64:96], in0=V3(eq), in1=V3(notown), op=ALU.mult),
                     reads=[eqb, cst], writes=[b01b])

            def prep2(h):
                s = h % 2
                for grp in range(4):
                    def emit(e, grp=grp):
                        last = None
                        for i in range(4):
                            tt = grp * 4 + i
                            last = e.matmul(out=ps[TBK][0:96, i * 128:(i + 1) * 128], lhsT=b01[:, tt, :], rhs=negI[:, :],
                                            start=True, stop=True)
                        return last
                    k.op("pe", emit, reads=[b01b, cst], writes=[psb[TBK]])
                    k.op("dve", lambda e, grp=grp: e.tensor_copy(out=Qaug[s][64:96, grp * 512:(grp + 1) * 512], in_=ps[TBK][64:96, :]),
                         reads=[psb[TBK]], writes=[Qab[s]])

            items = []
            for h in range(NH):
                for P in range(NLB // 2):
                    for mm in range(2 * P + 2):
                        for r in range(4):
                            for kt in range(2):
                                items.append((h, P, mm, r, kt))
            per_head = len(items) // NH
            SKEW = 3
            pend_pv = []
            deferred = []

            GRP = 2

            def do_score(i0):
                grp = [items[i0 + t] for t in range(GRP)]
                s = grp[0][0] % 2

                def emit(e):
                    last = None
                    for t, (h, P, mm, r, kt) in enumerate(grp):
                        bk = SB[(i0 + t) % 5]
                        q0 = P * 512 + (256 if mm == 2 * P + 1 else 0)
                        w = (P + 1) * 512 - q0
                        last = e.matmul(out=ps[bk][:, 0:w],
                                        lhsT=Kaug[s][0:96, r, mm * 256 + kt * 128:mm * 256 + (kt + 1) * 128],
                                        rhs=Qaug[s][0:96, q0:q0 + w], start=True, stop=True)
                    return last
                k.op("pe", emit, reads=[Kb[s], Qb[s], Qab[s]], writes=[psb[SB[(i0 + t) % 5]] for t in range(GRP)])
                for t, (h, P, mm, r, kt) in enumerate(grp):
                    bk = SB[(i0 + t) % 5]
                    p = (i0 + t) % 16
                    w = 256 if mm == 2 * P + 1 else 512
                    k.op("act", lambda e, bk=bk, p=p, w=w: e.activation(out=Pt[p][:, 0:w], in_=ps[bk][:, 0:w], func=AF.Exp, scale=1.0 / math.sqrt(HD)),
                         reads=[psb[bk]], writes=[Ptb[p]])
                    if mm >= 2 * P:
                        off = 0
                        k.op("dve", lambda e, p=p, off=off, r=r, kt=kt: e.tensor_tensor(
                            out=Pt[p][:, off:off + 256], in0=Pt[p][:, off:off + 256],
                            in1=masks[:, r, kt * 256:(kt + 1) * 256], op=ALU.mult),
                            reads=[Ptb[p], cst], writes=[Ptb[p]])

            def do_pv(i0):
                grp = [items[i0 + t] for t in range(GRP)]
                h, P = grp[0][0], grp[0][1]
                s = h % 2
                hp = h * (NLB // 2) + P
                ob = OBK[hp % 2]
                flags = []
                for (h_, P_, mm, r, kt) in grp:
                    assert (h_, P_) == (h, P)
                    flags.append(((mm == 0 and r == 0 and kt == 0), (mm == 2 * P + 1 and r == 3 and kt == 1)))

                def emit(e):
                    last = None
                    for t, (h_, P_, mm, r, kt) in enumerate(grp):
                        p = (i0 + t) % 16
                        o0 = 256 if mm == 2 * P + 1 else 0
                        last = e.matmul(out=ps[ob][0:65, o0:512], lhsT=Vaug[s][:, r, mm * 2 + kt, :], rhs=Pt[p][:, 0:512 - o0],
                                        start=flags[t][0], stop=flags[t][1])
                    return last
                k.op("pe", emit, reads=[Vb[s]] + [Ptb[(i0 + t) % 16] for t in range(GRP)], writes=[psb[ob]])
                if flags[-1][1]:
                    o2 = hp % 2
                    hs = h % 2
                    k.op("dve", lambda e: e.reciprocal(out=rs[64:65, :], in_=ps[ob][64:65, :]), reads=[psb[ob]], writes=[rsb])
                    k.op("dve", lambda e: e.tensor_copy(out=osb[o2][:, :], in_=ps[ob][0:64, :]),
                         reads=[psb[ob]], writes=[osbb[o2]])

                    def fin():
                        k.op("pe", lambda e: e.matmul(out=ps[BCK][0:64, :], lhsT=onesf[64:65, 0:64], rhs=rs[64:65, :],
                                                      start=True, stop=True),
                             reads=[rsb, cst], writes=[psb[BCK]])
                        k.op("dve", lambda e: e.tensor_tensor(out=ohead[hs][:, P * 512:(P + 1) * 512], in0=osb[o2][:, :],
                                                              in1=ps[BCK][0:64, :], op=ALU.mult),
                             reads=[osbb[o2], psb[BCK]], writes=[oheadb[hs]])
                        if P == NLB // 2 - 1:
                            k.dma("pool", oT[h * 64:(h + 1) * 64, :], ohead[hs][:, :], reads=[oheadb[hs]], writes=[oTb])
                    deferred.append([8, fin])

            def tick():
                for d in deferred:
                    d[0] -= 1
                while deferred and deferred[0][0] <= 0:
                    deferred.pop(0)[1]()

            prep1(0)
            prep1b(0)
            prep2(0)
            SKG = 4
            ngrp = len(items) // GRP
            for gi in range(ngrp):
                i = gi * GRP
                h = items[i][0]
                li = i - h * per_head
                if li == 24 and h == 1:
                    load_weight(2, wpb, wpbb)
                if li == 24 and h == 2:
                    load_weight(3, wo, wob)
                if li == 4 * SKG and h + 1 < NH:
                    prep1(h + 1)
                if li == 60 and h + 1 < NH:
                    prep1b(h + 1)
                if li == 112 and h + 1 < NH:
                    prep2(h + 1)
                do_score(i)
                if gi - SKG >= 0:
                    do_pv((gi - SKG) * GRP)
                tick()
            for gi in range(ngrp - SKG, ngrp):
                do_pv(gi * GRP)
                tick()
            while deferred:
                deferred.pop(0)[1]()
            k.barrier()

        with ExitStack() as st1:
            sb1 = lambda n, s, d: st1.enter_context(nc.sbuf_tensor(pfx + n, list(s), d))
            og = [sb1("og%d" % i, [128, 8, 512], BF16) for i in range(2)]
            sbzg = [sb1("sbzg%d" % i, [128, 8, 512], BF16) for i in range(2)]
            sgbg = [sb1("sgbg%d" % i, [128, 8, 512], BF16) for i in range(2)]
            OBg = [sb1("OBg%d" % i, [128, 8, 512], BF16) for i in range(2)]
            mg = [sb1("mg%d" % i, [128, 8, 512], BF16) for i in range(2)]
            ogb = [Buf() for _ in range(2)]
            sbzb = [Buf() for _ in range(2)]
            sgbb = [Buf() for _ in range(2)]
            OBgb = [Buf() for _ in range(2)]
            mgb = [[Buf() for _ in range(8)] for _ in range(2)]
            tmpf = [sb1("tmpf%d" % i, [128, 512], F32) for i in range(2)]
            tmpfb = [Buf() for _ in range(2)]
            xt = [sb1("xt%d" % i, [128, D], F32) for i in range(2)]
            xtb = [Buf() for _ in range(2)]
            Z = [sb1("Z%d" % i, [128, D], F32) for i in range(2)]
            Zb = [Buf() for _ in range(2)]
            Y = Z
            Yb = Zb
            lng = sb1("lng", [128, D], F32)
            lnb_t = sb1("lnb", [128, D], F32)
            stt = [sb1("stt%d" % i, [128, 2, 6], F32) for i in range(2)]
            mv = [sb1("mv%d" % i, [128, 2], F32) for i in range(2)]
            rsd = [sb1("rsd%d" % i, [128, 1], F32) for i in range(2)]
            sttb = [Buf() for _ in range(2)]
            mvb = [Buf() for _ in range(2)]
            rsdb = [Buf() for _ in range(2)]
            eps2 = sb1("eps2", [128, 1], F32)
            k.op("pool", lambda e: e.memset(eps2[:, :], LN_EPS), writes=[cst])
            k.dma("sp", lng[:, :], io["lng"][:, :], writes=[cst])
            k.dma("sp", lnb_t[:, :], io["lnb"][:, :], writes=[cst])
            y_out = io["y"]
            fl = lambda t: t[:, :, :].rearrange("p a b -> p (a b)")
            GS = [slice(g * 512, (g + 1) * 512) for g in range(4)]

            def stage_L(g):
                q2 = g % 2
                gs = GS[g]
                k.dma("sp", og[q2][:, :, :], oT.rearrange("(c p) t -> p c t", p=128)[:, :, gs], reads=[oTb], writes=[ogb[q2]])
                k.dma("sp", sbzg[q2][:, :, :], io["sbz"].rearrange("(c p) t -> p c t", p=128)[:, :, gs], writes=[sbzb[q2]])
                k.dma("sp", sgbg[q2][:, :, :], io["sgb"].rearrange("(c p) t -> p c t", p=128)[:, :, gs], writes=[sgbb[q2]])
                k.op("pool", lambda e: e.tensor_tensor(out=fl(OBg[q2]), in0=fl(og[q2]), in1=fl(sbzg[q2]), op=ALU.mult),
                     reads=[ogb[q2], sbzb[q2]], writes=[OBgb[q2]])

            def stage_P(g, c2):
                q2 = g % 2
                gs = GS[g]
                bk = c2 % 3
                s = c2 % 2

                def emit(e):
                    last = None
                    for kk in range(8):
                        last = e.matmul(out=ps[bk][:, :], lhsT=wpb[:, kk, c2 * 128:(c2 + 1) * 128], rhs=OBg[q2][:, kk, :],
                                        start=(kk == 0), stop=(kk == 7))
                    return last
                k.op("pe", emit, reads=[wpbb, OBgb[q2]], writes=[psb[bk]])
                k.op("dve", lambda e: e.tensor_tensor(out=tmpf[s][:, :], in0=ps[bk][:, :], in1=sgbg[q2][:, c2, :], op=ALU.mult),
                     reads=[psb[bk], sgbb[q2]], writes=[tmpfb[s]])
                k.op("pool", lambda e: e.tensor_tensor(out=mg[q2][:, c2, :], in0=tmpf[s][:, :], in1=Ma[:, c2, gs], op=ALU.add),
                     reads=[tmpfb[s], Mab[g]], writes=[mgb[q2][c2]])

            def stage_O(g, t4):
                q2 = g % 2
                tt = g * 4 + t4
                s = tt % 2
                k.dma("sp", xt[s][:, :], io["x"][tt * 128:(tt + 1) * 128, :], writes=[xtb[s]])
                for half in range(2):
                    bk = 3 + (tt * 2 + half) % 4
                    hs_ = slice(half * 512, (half + 1) * 512)

                    def emit(e, hs_=hs_, bk=bk):
                        last = None
                        for kk in range(8):
                            last = e.matmul(out=ps[bk][:, :], lhsT=mg[q2][:, kk, t4 * 128:(t4 + 1) * 128], rhs=wo[:, kk, hs_],
                                            start=(kk == 0), stop=(kk == 7))
                        return last
                    k.op("pe", emit, reads=[wob] + mgb[q2], writes=[psb[bk]])
                    k.op("dve", lambda e, hs_=hs_, bk=bk: e.scalar_tensor_tensor(
                        out=Z[s][:, hs_], in0=xt[s][:, hs_], scalar=float(ALPHA), in1=ps[bk][:, :], op0=ALU.mult, op1=ALU.add),
                        reads=[xtb[s], psb[bk]], writes=[Zb[s]])
                    k.op("dve", lambda e, hs_=hs_, half=half: e.bn_stats(out=stt[s][:, half, :], in_=Z[s][:, hs_]),
                         reads=[Zb[s]], writes=[sttb[s]])
                k.op("dve", lambda e: e.bn_aggr(out=mv[s][:, :], in_=stt[s][:, :, :].rearrange("p a b -> p (a b)")),
                     reads=[sttb[s]], writes=[mvb[s]])
                k.op("act", lambda e: e.activation(out=rsd[s][:, :], in_=mv[s][:, 1:2], func=AF.Sqrt, bias=eps2[:, 0:1], scale=1.0),
                     reads=[mvb[s], cst], writes=[rsdb[s]])
                k.op("dve", lambda e: e.reciprocal(out=rsd[s][:, :], in_=rsd[s][:, :]), reads=[rsdb[s]], writes=[rsdb[s]])
                k.op("dve", lambda e: e.tensor_scalar(out=Y[s][:, :], in0=Z[s][:, :], scalar1=mv[s][:, 0:1], scalar2=rsd[s][:, 0:1],
                                                      op0=ALU.subtract, op1=ALU.mult),
                     reads=[Zb[s], mvb[s], rsdb[s]], writes=[Yb[s]])
                k.op("dve", lambda e: e.tensor_tensor(out=Y[s][:, :], in0=Y[s][:, :], in1=lng[:, :], op=ALU.mult),
                     reads=[Yb[s], cst], writes=[Yb[s]])
                k.op("pool", lambda e: e.tensor_tensor(out=Y[s][:, :], in0=Y[s][:, :], in1=lnb_t[:, :], op=ALU.add),
                     reads=[Yb[s], cst], writes=[Yb[s]])
                k.dma("pool", y_out[tt * 128:(tt + 1) * 128, :], Y[s][:, :], reads=[Yb[s]])

            stage_L(0)
            for c2 in range(8):
                stage_P(0, c2)
            for g in range(4):
                if g + 1 < 4:
                    stage_L(g + 1)
                for t4 in range(4):
                    stage_O(g, t4)
                    if g + 1 < 4:
                        stage_P(g + 1, 2 * t4)
                        stage_P(g + 1, 2 * t4 + 1)
            k.barrier()

def core_consts(j):
    sel4 = np.zeros((128, 4), np.float32)
    sel4[:, (j - 1) % 4] = 1.0
    npr = np.arange(32)
    nglob = 4 * (npr % 8) + npr // 8
    pastb = np.zeros((16, 32), np.float32)
    notown = np.ones((16, 32), np.float32)
    for tt in range(16):
        own = 4 * (tt // 2) + j
        pastb[tt, nglob >= own] = -1e30
        notown[tt, nglob == own] = 0.0
    pastb = np.broadcast_to(pastb.reshape(1, 512), (128, 512)).copy()
    notown = np.broadcast_to(notown.reshape(1, 512), (128, 512)).copy()
    masks = np.zeros((128, 4, 2, 256), np.float32)
    p = np.arange(128)[:, None]
    q = np.arange(256)[None, :]
    for r in range(4):
        if r < j:
            masks[:, r] = 1.0
        elif r == j:
            for kt in range(2):
                masks[:, r, kt] = ((kt * 128 + p) <= q).astype(np.float32)
    return dict(sel4=sel4, pastb=pastb, notown=notown, masks=_bf(masks.reshape(128, 4, 512)))


def shared_consts():
    ind = np.zeros((32, 4, TL), np.float32)
    for r in range(4):
        for mm in range(8):
            ind[r * 8 + mm, r, mm * BLK:(mm + 1) * BLK] = 1.0
    return dict(ind=_bf(ind.reshape(32, 4 * TL)), ident_b=_bf(np.eye(128, dtype=np.float32)),
                negI=_bf(np.eye(128, dtype=np.float32) * NEGBIG))


def prep_B_weights(conv_w, conv_b, cln_g, cln_b, w_pw2, w_proj_a, w_proj_b, w_out, ln_g, ln_b):
    cw = np.ascontiguousarray(conv_w.T.reshape(8, 128, CK).transpose(1, 0, 2), dtype=np.float32)
    cvec = np.stack([v.reshape(8, 128).T for v in (conv_b, cln_g, cln_b)], axis=1)
    cvec = np.ascontiguousarray(cvec, dtype=np.float32)
    w4 = np.stack([w.reshape(8, 128, D).transpose(1, 0, 2) for w in (w_pw2, w_proj_a, w_proj_b, w_out)])
    w4 = np.ascontiguousarray(w4, dtype=np.float32)
    lng = np.ascontiguousarray(np.broadcast_to(ln_g[None, :], (128, D)), dtype=np.float32)
    lnb = np.ascontiguousarray(np.broadcast_to(ln_b[None, :], (128, D)), dtype=np.float32)
    return dict(cw=cw, cvec=cvec, w4=w4, lng=lng, lnb=lnb)


A_IN_L = [("w_in_r", [N_WCH, 128, 1024], F32), ("bias_col", [128, N_ACH], F32), ("b_v", [1, D], F32)]
B_IN_L = [("cw", [128, 8, CK], F32), ("cvec", [128, 3, 8], F32), ("w4", [4, 128, 8, D], F32),
          ("lng", [128, D], F32), ("lnb", [128, D], F32)]
SHARED_IN = [("x", [TL, D], F32), ("cos", [128, TL], F32), ("sins", [128, TL], F32), ("ident_f", [128, 128], F32),
             ("ident_b", [128, 128], BF16), ("negI", [128, 128], BF16), ("ind", [32, 4 * TL], BF16),
             ("sel4", [128, 4], F32), ("pastb", [128, 512], F32), ("notown", [128, 512], F32), ("masks", [128, 4, 512], BF16)]
GROUPS = [[0, 1, 2, 3], [4, 5, 6, 7]]


def build_fused():
    nc = bass.Bass("TRN2", target_bir_lowering=False)
    ext = {}
    for n, s, d in SHARED_IN:
        ext[n] = nc.dram_tensor(n, list(s), d, kind="ExternalInput").ap()
    for l in range(DEPTH):
        for n, s, d in A_IN_L + B_IN_L:
            ext["%s%d" % (n, l)] = nc.dram_tensor("%s%d" % (n, l), list(s), d, kind="ExternalInput").ap()
    y = nc.dram_tensor("y", [TL, D], F32, kind="ExternalOutput").ap()
    scr = lambda n, s, d: nc.dram_tensor(n, list(s), d).ap()
    with ExitStack() as st:
        k = K(nc, st)
        ccsem = st.enter_context(nc.semaphore("ccsem"))
        ccsem_h = st.enter_context(nc.semaphore("ccsem_h"))
        ccn = 0
        xcur = ext["x"]
        for l in range(DEPTH):
            sfx = "_%d" % l
            own = {n: scr(n + sfx, [D, TL], BF16) for n in ("qT", "hT", "saz", "sbz", "sga", "sgb")}
            own["qlo"] = scr("qlo" + sfx, [D, TL], BF16)
            own["kmean"] = scr("kmean" + sfx, [D, NLB], F32)
            own["htail"] = scr("htail" + sfx, [D, NLB * HALO], BF16)
            KTp = [scr("KT%d" % i + sfx, [256, TL], BF16) for i in range(4)]
            Vpp = [scr("Vp%d" % i + sfx, [128, 2 * 1040], BF16) for i in range(8)]
            ioA = dict(own)
            ioA["KT_rows"] = lambda c, KTp=KTp: KTp[c // 2][(c % 2) * 128:(c % 2 + 1) * 128, :]
            ioA["Vp_piece"] = lambda pc, Vpp=Vpp: Vpp[pc][:, :]
            ioA.update(x=xcur, cos=ext["cos"], sins=ext["sins"], ident_f=ext["ident_f"])
            for n, _, _ in A_IN_L:
                ioA[n] = ext["%s%d" % (n, l)]
            ioB = {}

            def gather(src, name, shp, dt, sfx, first=False):
                nonlocal ccn
                dst = scr(name + sfx, shp, dt)
                if first:
                    inst = nc.gpsimd.collective_compute("AllGather", ALU.bypass, replica_groups=GROUPS,
                                                        ins=[src.opt()], outs=[dst.opt()], dma_qos="P3")
                    inst.then_inc(ccsem_h)
                else:
                    inst = nc.gpsimd.collective_compute("AllGather", ALU.bypass, replica_groups=GROUPS,
                                                        ins=[src.opt()], outs=[dst.opt()], dma_qos="P3")
                    inst.then_inc(ccsem)
                    ccn += 1
                return dst

            def hook(toks, sfx=sfx, own=own, ioB=ioB):
                k._wait("pool", [(t, "raw") for t in toks])
                ioB["htail_all"] = gather(own["htail"], "htall", [4 * D, NLB * HALO], BF16, sfx, first=True)

            def kv_gather(toks, sfx=sfx, own=own, KTp=KTp, Vpp=Vpp, ioB=ioB):
                k._wait("pool", [(t, "raw") for t in toks])
                ioB["km_all"] = gather(own["kmean"], "kmall", [4 * D, NLB], F32, sfx)
                KTa = [gather(KTp[i], "KTall%d" % i, [4 * 256, TL], BF16, sfx) for i in range(4)]
                Vpa = [gather(Vpp[i], "Vpall%d" % i, [512, 2 * 1040], BF16, sfx) for i in range(8)]
                ioB["KT_all_head"] = lambda r, h, KTa=KTa: KTa[h // 4][r * 256 + (h % 4) * 64:r * 256 + (h % 4 + 1) * 64, :]
                ioB["Vp_all_head"] = lambda h, Vpa=Vpa: Vpa[h // 2].rearrange("(r p) f -> p r f", p=128)[:, :, (h % 2) * 1040:(h % 2 + 1) * 1040]
            ioA["hook_kv"] = kv_gather
            ioA["hook"] = hook
            phase_A(k, ioA, "a%d_" % l)
            for en in k.E:
                k.E[en].wait_ge(ccsem_h, l + 1)

            def pre_attn():
                for en in k.E:
                    k.E[en].wait_ge(ccsem, ccn)
            ioB["pre_attn"] = pre_attn
            for n in ("qT", "qlo", "hT", "saz", "sbz", "sga", "sgb"):
                ioB[n] = own[n]
            for n, _, _ in B_IN_L:
                ioB[n] = ext["%s%d" % (n, l)]
            for n in ("ident_b", "negI", "ind", "sel4", "pastb", "notown", "masks"):
                ioB[n] = ext[n]
            ioB["x"] = xcur
            ioB["oT"] = scr("oT" + sfx, [D, TL], BF16)
            if l == DEPTH - 1:
                ioB["y"] = y
            else:
                ioB["y"] = scr("xnext" + sfx, [TL, D], F32)
            phase_B(k, ioB, "b%d_" % l)
            xcur = ioB["y"]
    return nc


def kernel(x, w_in, b_in, conv_w, conv_b, conv_ln_g, conv_ln_b, w_pw2,
           w_proj_a, w_proj_b, w_out, ln_g, ln_b):
    x = np.asarray(x, dtype=np.float32)
    f = lambda a: np.asarray(a, dtype=np.float32)
    shared = {}
    for l in range(DEPTH):
        w_in_r, bias_col, b_v = prep_A_weights(f(w_in)[l], f(b_in)[l])
        shared["w_in_r%d" % l] = w_in_r
        shared["bias_col%d" % l] = bias_col
        shared["b_v%d" % l] = b_v
        wB = prep_B_weights(f(conv_w)[l], f(conv_b)[l], f(conv_ln_g)[l], f(conv_ln_b)[l], f(w_pw2)[l],
                            f(w_proj_a)[l], f(w_proj_b)[l], f(w_out)[l], f(ln_g)[l], f(ln_b)[l])
        for n, v in wB.items():
            shared["%s%d" % (n, l)] = v
    shared.update(shared_consts())
    shared["ident_f"] = np.eye(128, dtype=np.float32)
    in_maps = []
    for c in range(8):
        b, j = c // 4, c % 4
        m = dict(shared)
        m["x"] = np.ascontiguousarray(x[b][core_tokens(j)])
        cos, sins = rope_tables(j)
        m["cos"] = cos
        m["sins"] = sins
        m.update(core_consts(j))
        in_maps.append(m)
    nc = build_fused()
    res = run_bass_kernel_spmd(nc, in_maps, core_ids=list(range(8)))
    out = np.zeros((B, S, D), np.float32)
    for c in range(8):
        b, j = c // 4, c % 4
        out[b][core_tokens(j)] = np.asarray(res.results[c]["y"])
    return out
```
